# Optimizing a Trainium2 kernel written in Bass

```python
import math
import jax, jax.numpy as jnp
from jax import lax
import numpy as np

D_MODEL = 1024
BATCH = 8
SEQ = 4096
DEPTH = 2

HEAD_DIM = 64
BLOCK = 128
A_HEADS = 4
A_QK = 2 * HEAD_DIM
A_V = 2 * HEAD_DIM
A_WIDTH = A_HEADS * A_V
B_HEADS = 4
B_KV = 2
WINDOW = 128
B_WIDTH = B_HEADS * HEAD_DIM
C_HEADS = 4
C_KV = 2
C_WIDTH = C_HEADS * HEAD_DIM
GRID_W = 64
ROPE_THETA = 10000.0
MIX_WIDTH = A_WIDTH + B_WIDTH + C_WIDTH
SPLIT_SIZES = (A_HEADS * A_QK, A_HEADS * A_QK, A_HEADS * A_V,
               B_HEADS * HEAD_DIM, B_KV * HEAD_DIM, B_KV * HEAD_DIM,
               C_HEADS * HEAD_DIM, C_KV * HEAD_DIM, C_KV * HEAD_DIM)
IN_WIDTH = sum(SPLIT_SIZES)
D_FF = 4 * D_MODEL
EPS = 1e-6
N_ALIBI = A_HEADS + B_HEADS
ALIBI_SLOPES = tuple(2.0 ** (-8.0 * (i + 1) / N_ALIBI) for i in range(N_ALIBI))
A_SLOPES = ALIBI_SLOPES[0::2]
B_SLOPES = ALIBI_SLOPES[1::2]

kernel_name = "hybrid_parallel_diff_window_axial_encoder"


def _split_points():
    pts, acc = [], 0
    for s in SPLIT_SIZES[:-1]:
        acc += s
        pts.append(acc)
    return pts


def rms_norm(x, g):
    xf = x.astype(jnp.float32)
    y = xf * lax.rsqrt(jnp.mean(xf * xf, axis=-1, keepdims=True) + EPS)
    return (y * g.astype(jnp.float32)).astype(x.dtype)


def diff_attention(q, k, v, lam, lam_init, subln_g):
    bsz, s_len = q.shape[0], q.shape[1]
    nb = s_len // BLOCK
    scale = HEAD_DIM ** -0.5
    slopes = jnp.asarray(A_SLOPES, dtype=jnp.float32)
    kpos = jnp.arange(s_len)
    qb = q.reshape(bsz, nb, BLOCK, A_HEADS, 2, HEAD_DIM).transpose(1, 0, 2, 3, 4, 5)

    def block(args):
        qi, i = args
        sc = jnp.einsum('bqhcd,bshcd->bchqs', qi, k).astype(jnp.float32) * scale
        qpos = i * BLOCK + jnp.arange(BLOCK)
        dist = jnp.abs(qpos[:, None] - kpos[None, :]).astype(jnp.float32)
        sc = sc - slopes[:, None, None] * dist[None]
        p = jax.nn.softmax(sc, axis=-1)
        attn = p[:, 0] - lam * p[:, 1]
        return jnp.einsum('bhqs,bshe->bqhe', attn.astype(v.dtype), v)

    o = lax.map(block, (qb, jnp.arange(nb)))
    o = o.transpose(1, 0, 2, 3, 4).reshape(bsz, s_len, A_HEADS, A_V)
    o = rms_norm(o, subln_g) * (1.0 - lam_init)
    return o.reshape(bsz, s_len, A_WIDTH)


def window_attention(q, k, v, sink):
    bsz, s_len = q.shape[0], q.shape[1]
    nb = s_len // BLOCK
    grp = B_HEADS // B_KV
    scale = HEAD_DIM ** -0.5
    qb = q.reshape(bsz, nb, BLOCK, B_KV, grp, HEAD_DIM)

    def band(t):
        tp = jnp.pad(t, ((0, 0), (BLOCK, BLOCK), (0, 0), (0, 0)))
        tb = tp.reshape(bsz, nb + 2, BLOCK, B_KV, HEAD_DIM)
        return jnp.concatenate([tb[:, :-2], tb[:, 1:-1], tb[:, 2:]], axis=2)

    kw, vw = band(k), band(v)
    sc = jnp.einsum('bnqkgd,bnskd->bnkgqs', qb, kw).astype(jnp.float32) * scale
    j = jnp.arange(3 * BLOCK)
    a = jnp.arange(BLOCK)
    delta = j[None, :] - BLOCK - a[:, None]
    spos = jnp.arange(nb)[:, None] * BLOCK - BLOCK + j[None, :]
    mask = (jnp.abs(delta) <= WINDOW)[None] & ((spos >= 0) & (spos < s_len))[:, None, :]
    slopes = jnp.asarray(B_SLOPES, dtype=jnp.float32).reshape(B_KV, grp)
    bias = -slopes[:, :, None, None] * jnp.abs(delta).astype(jnp.float32)[None, None]
    sc = jnp.where(mask[None, :, None, None], sc + bias[None, None], -jnp.inf)
    sk = jnp.broadcast_to(sink.astype(jnp.float32).reshape(B_KV, grp)[None, None, :, :, None, None],
                          sc.shape[:-1] + (1,))
    p = jax.nn.softmax(jnp.concatenate([sc, sk], axis=-1), axis=-1)[..., :-1]
    o = jnp.einsum('bnkgqs,bnskd->bnqkgd', p.astype(v.dtype), vw)
    return o.reshape(bsz, s_len, B_WIDTH)


def axial_rope_tables(s_len):
    rows = s_len // GRID_W
    row = jnp.repeat(jnp.arange(rows), GRID_W).astype(jnp.float32)
    col = jnp.tile(jnp.arange(GRID_W), rows).astype(jnp.float32)
    half = HEAD_DIM // 2
    freqs = ROPE_THETA ** (-jnp.arange(0, half, 2, dtype=jnp.float32) / half)
    ar = row[:, None] * freqs[None]
    ac = col[:, None] * freqs[None]
    return jnp.cos(ar), jnp.sin(ar), jnp.cos(ac), jnp.sin(ac)


def _rotate(x, cos, sin):
    x1, x2 = jnp.split(x, 2, axis=-1)
    c, s = cos[:, None, :], sin[:, None, :]
    return jnp.concatenate([x1 * c - x2 * s, x2 * c + x1 * s], axis=-1)


def apply_axial_rope(x, tabs):
    cr, sr, cc, scol = tabs
    xf = x.astype(jnp.float32)
    half = HEAD_DIM // 2
    y = jnp.concatenate([_rotate(xf[..., :half], cr, sr), _rotate(xf[..., half:], cc, scol)], axis=-1)
    return y.astype(x.dtype)


def grid_attention(q, k, v):
    bsz, s_len = q.shape[0], q.shape[1]
    nb = s_len // BLOCK
    grp = C_HEADS // C_KV
    scale = HEAD_DIM ** -0.5
    qb = q.reshape(bsz, nb, BLOCK, C_KV, grp, HEAD_DIM).transpose(1, 0, 2, 3, 4, 5)

    def block(qi):
        sc = jnp.einsum('bqkgd,bskd->bkgqs', qi, k).astype(jnp.float32) * scale
        p = jax.nn.softmax(sc, axis=-1)
        return jnp.einsum('bkgqs,bskd->bqkgd', p.astype(v.dtype), v)

    o = lax.map(block, qb)
    return o.transpose(1, 0, 2, 3, 4, 5).reshape(bsz, s_len, C_WIDTH)


def setup_inputs(seed: int = 0) -> dict:
    key = jax.random.key(seed)
    ks = jax.random.split(key, 20)
    f32 = jnp.float32

    def nrm(k, shape, scale):
        return jax.random.normal(k, shape, f32) * scale

    def gain(k, n):
        return 1.0 + 0.02 * jax.random.normal(k, (DEPTH, n), f32)

    return {
        "x": jax.random.normal(ks[0], (BATCH, SEQ, D_MODEL), f32),
        "w_in": nrm(ks[1], (DEPTH, D_MODEL, IN_WIDTH), D_MODEL ** -0.5),
        "w_out": nrm(ks[2], (DEPTH, MIX_WIDTH, D_MODEL), MIX_WIDTH ** -0.5),
        "g_pre_mix": gain(ks[3], D_MODEL),
        "g_post_mix": gain(ks[4], D_MODEL),
        "lam_q1": nrm(ks[5], (DEPTH, HEAD_DIM), 0.1),
        "lam_k1": nrm(ks[6], (DEPTH, HEAD_DIM), 0.1),
        "lam_q2": nrm(ks[7], (DEPTH, HEAD_DIM), 0.1),
        "lam_k2": nrm(ks[8], (DEPTH, HEAD_DIM), 0.1),
        "diff_subln_g": gain(ks[9], A_V),
        "sink_logits": nrm(ks[10], (DEPTH, B_HEADS), 0.5),
        "c_q_norm": gain(ks[11], HEAD_DIM),
        "c_k_norm": gain(ks[12], HEAD_DIM),
        "g_pre_mlp": gain(ks[13], D_MODEL),
        "g_post_mlp": gain(ks[14], D_MODEL),
        "w_mlp_in": nrm(ks[15], (DEPTH, D_MODEL, D_FF), D_MODEL ** -0.5),
        "w_mlp_out": nrm(ks[16], (DEPTH, D_FF, D_MODEL), D_FF ** -0.5),
    }


def reference(x, w_in, w_out, g_pre_mix, g_post_mix, lam_q1, lam_k1, lam_q2, lam_k2,
              diff_subln_g, sink_logits, c_q_norm, c_k_norm, g_pre_mlp, g_post_mlp,
              w_mlp_in, w_mlp_out):
    bsz, s_len = x.shape[0], x.shape[1]
    tabs = axial_rope_tables(s_len)
    pts = _split_points()
    for l in range(DEPTH):
        h = rms_norm(x, g_pre_mix[l])
        proj = h @ w_in[l]
        aq, ak, av, bq, bk, bv, cq, ck, cv = jnp.split(proj, pts, axis=-1)

        lam_init = 0.8 - 0.6 * math.exp(-0.3 * l)
        lam = (jnp.exp(jnp.sum(lam_q1[l].astype(jnp.float32) * lam_k1[l].astype(jnp.float32)))
               - jnp.exp(jnp.sum(lam_q2[l].astype(jnp.float32) * lam_k2[l].astype(jnp.float32)))
               + lam_init)
        a_out = diff_attention(aq.reshape(bsz, s_len, A_HEADS, 2, HEAD_DIM),
                               ak.reshape(bsz, s_len, A_HEADS, 2, HEAD_DIM),
                               av.reshape(bsz, s_len, A_HEADS, A_V),
                               lam, lam_init, diff_subln_g[l])

        b_out = window_attention(bq.reshape(bsz, s_len, B_HEADS, HEAD_DIM),
                                 bk.reshape(bsz, s_len, B_KV, HEAD_DIM),
                                 bv.reshape(bsz, s_len, B_KV, HEAD_DIM),
                                 sink_logits[l])

        cqh = apply_axial_rope(rms_norm(cq.reshape(bsz, s_len, C_HEADS, HEAD_DIM), c_q_norm[l]), tabs)
        ckh = apply_axial_rope(rms_norm(ck.reshape(bsz, s_len, C_KV, HEAD_DIM), c_k_norm[l]), tabs)
        c_out = grid_attention(cqh, ckh, cv.reshape(bsz, s_len, C_KV, HEAD_DIM))

        mix = jnp.concatenate([a_out, b_out, c_out], axis=-1) @ w_out[l]
        x = x + rms_norm(mix, g_post_mix[l])

        h = rms_norm(x, g_pre_mlp[l])
        y = jnp.square(jax.nn.relu(h @ w_mlp_in[l])) @ w_mlp_out[l]
        x = x + rms_norm(y, g_post_mlp[l])
    return x
```

```python
import math
import numpy as np
import ml_dtypes
from contextlib import ExitStack
import concourse.bass as bass
import concourse.mybir as mybir
from concourse.bass_utils import run_bass_kernel_spmd

F32 = mybir.dt.float32
BF16 = mybir.dt.bfloat16
AF = mybir.ActivationFunctionType
ALU = mybir.AluOpType
AX = mybir.AxisListType

S = 4096
D = 1024
NTT = 32
NQ = 8
DFF = 4096
EPS = 1e-6
DEPTH = 2
NEG = -60000.0


def _consts():
    bf = ml_dtypes.bfloat16
    c = {}
    c["ident"] = np.eye(128, dtype=np.float32).astype(bf)
    pos = np.arange(S)
    lo = (pos % 256).astype(np.float32)
    hi = (pos - pos % 256).astype(np.float32)
    one = np.ones(S, np.float32)
    alq = np.zeros((4, 4, S), np.float32)
    alk = np.zeros((4, 4, S), np.float32)
    diag = np.zeros((4, 128, 128), np.float32)
    ab = np.abs(np.arange(128)[:, None] - np.arange(128)[None, :]).astype(np.float32)
    for h in range(4):
        m8 = 8.0 * 2.0 ** (-(2 * h + 1))
        alq[h] = np.stack([m8 * hi, one, m8 * lo, one])
        alk[h] = np.stack([one, -m8 * hi, one, -m8 * lo])
        diag[h] = -2.0 * m8 * np.maximum(np.arange(128)[None, :] - np.arange(128)[:, None], 0)
    c["alq"] = alq.astype(bf)
    c["alka"] = alk.astype(bf)
    c["alkb"] = (-alk).astype(bf)
    c["diaga"] = diag.astype(bf)
    bb = np.zeros((4, 128, 384), np.float32)
    b_ = np.arange(128)[:, None]
    a_ = np.arange(128)[None, :]
    for h in range(4):
        m8 = 8.0 * 2.0 ** (-(2 * h + 2))
        for ri, rel in enumerate((-1, 0, 1)):
            delta = rel * 128 + b_ - a_
            v = np.where(np.abs(delta) <= 128, -m8 * np.abs(delta), NEG)
            bb[h][:, ri * 128:(ri + 1) * 128] = v
    c["biasb"] = bb.astype(bf)
    row = (pos // 64).astype(np.float32)
    col = (pos % 64).astype(np.float32)
    freqs = (np.float32(10000.0) ** (-np.arange(0, 32, 2, dtype=np.float32) / np.float32(32))).astype(np.float32)
    ar = (row[:, None] * freqs[None]).astype(np.float32)
    ac = (col[:, None] * freqs[None]).astype(np.float32)
    rope = np.concatenate([np.cos(ar), np.sin(ar), np.cos(ac), np.sin(ac)], axis=1).astype(np.float32)
    c["rope"] = np.ascontiguousarray(rope.reshape(NTT, 128, 64).transpose(1, 0, 2))
    return c


CONST_SPECS = {
    "ident": ([128, 128], BF16), "alq": ([4, 4, S], BF16), "alka": ([4, 4, S], BF16), "alkb": ([4, 4, S], BF16),
    "diaga": ([4, 128, 128], BF16), "biasb": ([4, 128, 384], BF16), "rope": ([128, NTT, 64], F32),
}
PARAM_SPECS = {
    "w_in": [DEPTH, D, 2560], "w_out": [DEPTH, D, D], "g_pre_mix": [DEPTH, D], "g_post_mix": [DEPTH, D],
    "lam_q1": [DEPTH, 64], "lam_k1": [DEPTH, 64], "lam_q2": [DEPTH, 64], "lam_k2": [DEPTH, 64],
    "diff_subln_g": [DEPTH, 128], "sink_logits": [DEPTH, 4], "c_q_norm": [DEPTH, 64], "c_k_norm": [DEPTH, 64],
    "g_pre_mlp": [DEPTH, D], "g_post_mlp": [DEPTH, D], "w_mlp_in": [DEPTH, D, DFF], "w_mlp_out": [DEPTH, DFF, D],
}


class Eng:
    def __init__(self, nc, es, name, obj):
        self.name, self.o = name, obj
        self.sem = es.enter_context(nc.semaphore("pg_" + name))
        self.cnt = 0
        self.waited = {}
        self.last = None

    def wait(self, *toks):
        for tok in toks:
            if tok is None:
                continue
            sem, val = tok
            if self.waited.get(id(sem), 0) >= val:
                continue
            self.o.wait_ge(sem, val)
            self.waited[id(sem)] = val

    def mark(self, ins):
        self.cnt += 1
        ins.then_inc(self.sem, 1)
        self.last = (self.sem, self.cnt)
        return self.last

    def dep(self):
        if self.last is not None:
            self.wait(self.last)


class DmaSlot:
    def __init__(self, nc, es, name):
        self.sem = es.enter_context(nc.semaphore("dq_" + name))
        self.cnt = 0

    def start(self, eng, out, in_):
        eng.o.dma_start(out=out, in_=in_).then_inc(self.sem, 16)
        self.cnt += 16
        return (self.sem, self.cnt)

    def tok(self):
        return (self.sem, self.cnt) if self.cnt else None


def build(debug=False, layers=(0, 1), phases=("p1", "p3", "p4", "p5")):
    nc = bass.Bass("TRN2", target_bir_lowering=False)

    def din(name, shape, dt=F32):
        return nc.dram_tensor(name, list(shape), dt, kind="ExternalInput").ap()

    x = din("x", [S, D])
    P = {k: din(k, v) for k, v in PARAM_SPECS.items()}
    C = {k: din(k, v[0], v[1]) for k, v in CONST_SPECS.items()}
    y = nc.dram_tensor("y", [S, D], F32, kind="ExternalOutput").ap()
    skind = "ExternalOutput" if debug else "Internal"
    qkT = nc.dram_tensor("qkT", [1408, S], BF16, kind=skind).ap()
    vscr = nc.dram_tensor("vscr", [S, 768], BF16, kind=skind).ap()
    cqkT = nc.dram_tensor("cqkT", [384, S], BF16, kind=skind).ap()
    mixT = nc.dram_tensor("mixT", [1024, S], BF16, kind=skind).ap()

    with ExitStack() as top:
        PE = Eng(nc, top, "pe", nc.tensor)
        ACT = Eng(nc, top, "act", nc.scalar)
        DVE = Eng(nc, top, "dve", nc.vector)
        POOL = Eng(nc, top, "pool", nc.gpsimd)
        SP = Eng(nc, top, "sp", nc.sync)
        ENGS = [PE, ACT, DVE, POOL, SP]
        slots = []

        slot_cache = {}

        def slot(name):
            key = name.rsplit("_", 1)[0] if name[-1].isdigit() and "_p" in name else name
            if key not in slot_cache:
                slot_cache[key] = DmaSlot(nc, top, key)
                slots.append(slot_cache[key])
            return slot_cache[key]

        def sbt(es, name, shape, dt):
            return es.enter_context(nc.sbuf_tensor("sb_" + name, list(shape), dt))

        def pst(es, name, shape, dt):
            return es.enter_context(nc.psum_tensor("ps_" + name, list(shape), dt))

        ident = sbt(top, "ident", [128, 128], BF16)
        ones = sbt(top, "ones", [128, 128], BF16)
        onesp = sbt(top, "onesp", [128, 2, 128], BF16)
        cs = slot("const")
        t_ident = cs.start(SP, ident[:], C["ident"][:, :])
        DVE.mark(nc.vector.memset(ones[:], 1.0))
        DVE.mark(nc.vector.memset(onesp[:], 0.0))
        DVE.dep()
        DVE.mark(nc.vector.memset(onesp[:, 0, 0:64], 1.0))
        t_ones = DVE.mark(nc.vector.memset(onesp[:, 1, 64:128], 1.0))

        def barrier():
            toks = [e.last for e in (PE, ACT, DVE, POOL)] + [s.tok() for s in slots]
            for e in ENGS:
                e.wait(*toks)

        def phase1(l):
            xsrc = x if l == 0 else y
            with ExitStack() as es:
                n = lambda s_: f"{s_}_p1_{l}"
                hT = sbt(es, n("hT"), [128, 8, S], BF16)
                win = sbt(es, n("win"), [128, 8, 2560], BF16)
                gbc = sbt(es, n("gbc"), [128, D], F32)
                gqk = sbt(es, n("gqk"), [128, 384], F32)
                rope = sbt(es, n("rope"), [128, NTT, 64], F32)
                xt = [sbt(es, n(f"xt{i}"), [128, D], F32) for i in range(3)]
                hb = [sbt(es, n(f"hb{i}"), [128, D], BF16) for i in range(4)]
                junk = sbt(es, n("junk"), [128, D], F32)
                ss = sbt(es, n("ss"), [128, NTT], F32)
                sd = sbt(es, n("sd"), [128, NTT], F32)
                rstd = sbt(es, n("rstd"), [128, NTT], F32)
                stg = [sbt(es, n(f"stg{i}"), [128, 512], BF16) for i in range(4)]
                vst = [sbt(es, n(f"vst{i}"), [128, 768], BF16) for i in range(2)]
                cqf = [sbt(es, n(f"cqf{i}"), [128, 384], F32) for i in range(2)]
                csq = sbt(es, n("csq"), [128, 384], F32)
                css = sbt(es, n("css"), [128, NTT, 6], F32)
                csd = sbt(es, n("csd"), [128, NTT, 6], F32)
                crr = sbt(es, n("crr"), [128, NTT, 6], F32)
                cqns = [sbt(es, n(f"cqn{i}"), [128, 384], F32) for i in range(2)]
                tA = sbt(es, n("tA"), [128, 192], F32)
                tB = sbt(es, n("tB"), [128, 192], F32)
                crb = [sbt(es, n(f"crb{i}"), [128, 384], BF16) for i in range(2)]
                cst = [sbt(es, n(f"cst{i}"), [128, 3, 128], BF16) for i in range(2)]
                ptr = [pst(es, n(f"ptr{i}"), [128, D], BF16) for i in range(2)]
                pp = [pst(es, n(f"pp{i}"), [128, 512], F32) for i in range(4)]
                ptc = pst(es, n("ptc"), [128, 512], BF16)

                wsl = slot(n("w"))
                gsl = slot(n("g"))
                xs = [slot(n(f"x{i}")) for i in range(3)]
                sts = [slot(n(f"st{i}")) for i in range(4)]
                vss = [slot(n(f"vs{i}")) for i in range(2)]
                css_ = [slot(n(f"cs{i}")) for i in range(2)]

                wsl2 = slot(n("w2"))

                def wblk(sl_, a, b):
                    return sl_.start(POOL, win[:, :, a:b], P["w_in"][l, :, a:b].rearrange("(c p) n -> p c n", p=128))
                wblk(wsl, 1024, 1536)
                t_win_tm = wblk(wsl, 1920, 2560)
                wblk(wsl2, 0, 1024)
                t_win_fm = wblk(wsl2, 1536, 1920)
                gsl.start(SP, gbc[:], P["g_pre_mix"][l:l + 1, :].partition_broadcast(128))
                for hh in range(4):
                    gsl.start(SP, gqk[:, hh * 64:(hh + 1) * 64], P["c_q_norm"][l:l + 1, :].partition_broadcast(128))
                for hh in range(2):
                    gsl.start(SP, gqk[:, 256 + hh * 64:256 + (hh + 1) * 64],
                              P["c_k_norm"][l:l + 1, :].partition_broadcast(128))
                t_g = gsl.start(SP, rope[:], C["rope"][:, :, :])

                xfree = [None] * 3
                hbfree = [None] * 4
                hb_tok = {}
                ptrfree = [None] * 2
                ppfree = [None] * 4
                stgfree = [None] * 4
                vstfree = [None] * 2
                cstfree = [None] * 2
                crbfree = [None] * 2
                cqffree = [None] * 2
                ptcfree = [None]
                hT_tok = [None] * NTT
                st = {"pp": 0, "stg": 0, "ev": 0}

                def next_pp():
                    b = st["pp"] % 4
                    st["pp"] += 1
                    PE.wait(ppfree[b])
                    return b

                def evac_eng():
                    return ACT

                def copy_on(e, out, in_):
                    if e is ACT:
                        return ACT.mark(nc.scalar.activation(out=out, in_=in_, func=AF.Copy))
                    return DVE.mark(nc.vector.tensor_copy(out=out, in_=in_))

                def stepA(tt):
                    sl = tt % 3
                    SP.wait(xfree[sl])
                    t_x = xs[sl].start(SP, xt[sl][:], xsrc[tt * 128:(tt + 1) * 128, :])
                    ACT.wait(t_x)
                    t1 = ACT.mark(nc.scalar.activation(out=junk[:], in_=xt[sl][:], func=AF.Square,
                                                       accum_out=ss[:, tt:tt + 1]))
                    ACT.wait(t1)
                    t2 = ACT.mark(nc.scalar.activation(out=sd[:, tt:tt + 1], in_=ss[:, tt:tt + 1], func=AF.Sqrt,
                                                       scale=1.0 / D, bias=EPS))
                    DVE.wait(t2)
                    t3 = DVE.mark(nc.vector.reciprocal(out=rstd[:, tt:tt + 1], in_=sd[:, tt:tt + 1]))
                    DVE.wait(t3, hbfree[tt % 4], t_g, t_x)
                    t4 = DVE.mark(nc.vector.scalar_tensor_tensor(out=hb[tt % 4][:], in0=xt[sl][:],
                                                                 scalar=rstd[:, tt:tt + 1], in1=gbc[:],
                                                                 op0=ALU.mult, op1=ALU.mult))
                    xfree[sl] = t4
                    hb_tok[tt] = t4

                def transA(tt):
                    PE.wait(hb_tok[tt], ptrfree[tt % 2], t_ident)
                    for c in range(8):
                        ins = nc.tensor.transpose(ptr[tt % 2][:, c * 128:(c + 1) * 128],
                                                  hb[tt % 4][:, c * 128:(c + 1) * 128], ident[:])
                    t5 = PE.mark(ins)
                    hbfree[tt % 4] = t5
                    ACT.wait(t5)
                    t6 = ACT.mark(nc.scalar.activation(out=hT[:, :, tt * 128:(tt + 1) * 128],
                                                       in_=ptr[tt % 2][:].rearrange("p (c t) -> p c t", c=8),
                                                       func=AF.Copy))
                    ptrfree[tt % 2] = t6
                    hT_tok[tt] = t6

                def fm_cols(g):
                    if g < 4:
                        return g * 128
                    if g < 8:
                        return 512 + (g - 4) * 128
                    if g < 10:
                        return 1536 + (g - 8) * 128
                    return 1792

                def proj_fm(j, groups):
                    PE.wait(hT_tok[4 * j + 3], t_win_fm)
                    for g in groups:
                        b = next_pp()
                        c0 = fm_cols(g)
                        for c in range(8):
                            ins = nc.tensor.matmul(pp[b][:], lhsT=win[:, c, c0:c0 + 128],
                                                   rhs=hT[:, c, j * 512:(j + 1) * 512], start=(c == 0), stop=(c == 7))
                        tP = PE.mark(ins)
                        ev = evac_eng()
                        s_ = st["stg"] % 4
                        st["stg"] += 1
                        ev.wait(tP, stgfree[s_])
                        tE = copy_on(ev, stg[s_][:], pp[b][:])
                        ppfree[b] = tE
                        POOL.wait(tE)
                        stgfree[s_] = sts[s_].start(POOL, qkT[g * 128:(g + 1) * 128, j * 512:(j + 1) * 512], stg[s_][:])

                def proj_tm(tt):
                    PE.wait(hT_tok[tt], t_win_tm)
                    lt = lambda c: hT[:, c, tt * 128:(tt + 1) * 128]
                    b1 = next_pp()
                    for c in range(8):
                        ins = nc.tensor.matmul(pp[b1][:], lhsT=lt(c), rhs=win[:, c, 1024:1536], start=(c == 0), stop=(c == 7))
                    tP1 = PE.mark(ins)
                    b2 = next_pp()
                    for c in range(8):
                        nc.tensor.matmul(pp[b2][:, 0:128], lhsT=lt(c), rhs=win[:, c, 1920:2048], start=(c == 0), stop=(c == 7))
                    for c in range(8):
                        ins = nc.tensor.matmul(pp[b2][:, 128:256], lhsT=lt(c), rhs=win[:, c, 2432:2560], start=(c == 0), stop=(c == 7))
                    tP2 = PE.mark(ins)
                    b3 = next_pp()
                    for c in range(8):
                        ins = nc.tensor.matmul(pp[b3][:, 0:384], lhsT=lt(c), rhs=win[:, c, 2048:2432], start=(c == 0), stop=(c == 7))
                    tP3 = PE.mark(ins)
                    vs = tt % 2
                    ACT.wait(tP1, vstfree[vs])
                    tE1 = ACT.mark(nc.scalar.activation(out=vst[vs][:, 0:512], in_=pp[b1][:], func=AF.Copy))
                    ppfree[b1] = tE1
                    ACT.wait(tP2)
                    tE2 = ACT.mark(nc.scalar.activation(out=vst[vs][:, 512:768], in_=pp[b2][:, 0:256], func=AF.Copy))
                    ppfree[b2] = tE2
                    POOL.wait(tE1, tE2)
                    vstfree[vs] = vss[vs].start(POOL, vscr[tt * 128:(tt + 1) * 128, :], vst[vs][:])
                    cf = cqf[tt % 2]
                    cqn = cqns[tt % 2]
                    cqnfree = [cqn_done[tt % 2]]
                    ACT.wait(tP3, cqffree[tt % 2])
                    tc0 = ACT.mark(nc.scalar.activation(out=cf[:], in_=pp[b3][:, 0:384], func=AF.Copy))
                    ppfree[b3] = tc0
                    V = nc.vector
                    DVE.wait(tc0)
                    DVE.dep()
                    DVE.mark(V.tensor_tensor(out=csq[:], in0=cf[:], in1=cf[:], op=ALU.mult))
                    DVE.dep()
                    tc2 = DVE.mark(V.tensor_reduce(out=css[:, tt, :], in_=csq[:].rearrange("p (h d) -> p h d", d=64),
                                                   axis=AX.X, op=ALU.add))
                    ACT.wait(tc2)
                    tc3 = ACT.mark(nc.scalar.activation(out=csd[:, tt, :], in_=css[:, tt, :], func=AF.Sqrt,
                                                       scale=1.0 / 64, bias=EPS))
                    DVE.wait(tc3)
                    DVE.mark(V.reciprocal(out=crr[:, tt, :], in_=csd[:, tt, :]))
                    DVE.dep()
                    rb = crr[:, tt, :].rearrange("p (h o) -> p h o", o=1).broadcast_to([128, 6, 64])
                    DVE.wait(cqnfree[0])
                    DVE.mark(V.tensor_tensor(out=cqn[:].rearrange("p (h d) -> p h d", d=64),
                                             in0=cf[:].rearrange("p (h d) -> p h d", d=64), in1=rb, op=ALU.mult))
                    DVE.dep()
                    DVE.wait(t_g)
                    tcn = DVE.mark(V.tensor_tensor(out=cqn[:], in0=cqn[:], in1=gqk[:], op=ALU.mult))
                    cqffree[tt % 2] = tcn
                    X = cqn[:].rearrange("p (h f t d) -> p h f t d", h=6, f=2, t=2, d=16)
                    x1, x2 = X[:, :, :, 0, :], X[:, :, :, 1, :]
                    R = rope[:, tt, :].rearrange("p (o f k d) -> p o f k d", o=1, f=2, k=2, d=16)
                    cosb = R[:, :, :, 0, :].broadcast_to([128, 6, 2, 16])
                    sinb = R[:, :, :, 1, :].broadcast_to([128, 6, 2, 16])
                    cb = crb[tt % 2]
                    Y = cb[:].rearrange("p (h f t d) -> p h f t d", h=6, f=2, t=2, d=16)
                    y1, y2 = Y[:, :, :, 0, :], Y[:, :, :, 1, :]
                    TA = tA[:].rearrange("p (h f d) -> p h f d", h=6, f=2, d=16)
                    TB = tB[:].rearrange("p (h f d) -> p h f d", h=6, f=2, d=16)
                    G = nc.gpsimd
                    POOL.wait(tcn, t_g, crbfree[tt % 2], cqnfree[0])
                    POOL.dep()
                    POOL.mark(G.tensor_tensor(out=TA, in0=x1, in1=cosb, op=ALU.mult))
                    POOL.mark(G.tensor_tensor(out=TB, in0=x2, in1=sinb, op=ALU.mult))
                    POOL.dep()
                    POOL.mark(G.tensor_tensor(out=y1, in0=TA, in1=TB, op=ALU.subtract))
                    POOL.dep()
                    POOL.mark(G.tensor_tensor(out=TA, in0=x2, in1=cosb, op=ALU.mult))
                    POOL.mark(G.tensor_tensor(out=TB, in0=x1, in1=sinb, op=ALU.mult))
                    POOL.dep()
                    tcr = POOL.mark(G.tensor_tensor(out=y2, in0=TA, in1=TB, op=ALU.add))
                    cqn_done[tt % 2] = tcr
                    pend_c.append((tt, tcr, cb))

                def proj_tm_finish():
                    tt, tcr, cb = pend_c.pop(0)
                    PE.wait(tcr, ptcfree[0])
                    for g in range(3):
                        ins = nc.tensor.transpose(ptc[:, g * 128:(g + 1) * 128], cb[:, g * 128:(g + 1) * 128], ident[:])
                    tct = PE.mark(ins)
                    crbfree[tt % 2] = tct
                    cs_ = tt % 2
                    ACT.wait(tct, cstfree[cs_])
                    tcc = ACT.mark(nc.scalar.activation(out=cst[cs_][:], in_=ptc[:, 0:384].rearrange("p (g t) -> p g t", g=3),
                                                       func=AF.Copy))
                    ptcfree[0] = tcc
                    POOL.wait(tcc)
                    cstfree[cs_] = css_[cs_].start(
                        POOL, cqkT[:, tt * 128:(tt + 1) * 128].rearrange("(g p) t -> p g t", p=128), cst[cs_][:])

                pend_c = []
                cqn_done = [None, None]
                fm_split = [(0, 1, 2), (3, 4, 5), (6, 7, 8), (9, 10)]
                for step in range(NQ + 1):
                    if step < NQ:
                        for tt in range(4 * step, 4 * step + 4):
                            stepA(tt)
                    if step >= 1:
                        j = step - 1
                        for i_, tt in enumerate(range(4 * j, 4 * j + 4)):
                            proj_tm(tt)
                            if len(pend_c) > 1:
                                proj_tm_finish()
                            proj_fm(j, fm_split[i_])
                    if step < NQ:
                        for tt in range(4 * step, 4 * step + 4):
                            transA(tt)
                while pend_c:
                    proj_tm_finish()
                barrier()


        def phase3(l):
            lam_init = 0.8 - 0.6 * math.exp(-0.3 * l)
            with ExitStack() as es:
                n = lambda s_: f"{s_}_p3_{l}"
                Q = [sbt(es, n(f"Q{i}"), [128, 2, S], BF16) for i in range(2)]
                Kb = [sbt(es, n(f"K{i}"), [128, 4, S], BF16) for i in range(2)]
                Vb = [sbt(es, n(f"V{i}"), [128, NTT, 256], BF16) for i in range(2)]
                NE = 5
                et = [sbt(es, n(f"et{i}"), [128, 1024], BF16) for i in range(NE)]
                zsum = [sbt(es, n(f"zsum{i}"), [128, 512], F32) for i in range(2)]
                zhi = [sbt(es, n(f"zhi{i}"), [128, 512], BF16) for i in range(2)]
                zlo = [sbt(es, n(f"zlo{i}"), [128, 512], BF16) for i in range(2)]
                rz = sbt(es, n("rz"), [128, 512], F32)
                lz = sbt(es, n("lz"), [128, 512], F32)
                ls = sbt(es, n("ls"), [128, 512], F32)
                rsb = sbt(es, n("rsb"), [128, 512], F32)
                o1 = sbt(es, n("o1"), [128, 512], F32)
                o2 = sbt(es, n("o2"), [128, 512], F32)
                obuf = [sbt(es, n(f"obuf{i}"), [128, 512], F32) for i in range(2)]
                sqb = [sbt(es, n(f"sqb{i}"), [128, 512], BF16) for i in range(2)]
                obs = [sbt(es, n(f"obs{i}"), [128, 512], BF16) for i in range(2)]
                diag = sbt(es, n("diag"), [128, 4, 128], BF16)
                biasb = sbt(es, n("biasb"), [128, 4, 384], BF16)
                lq = [sbt(es, n(f"lq{i}"), [128, 64], F32) for i in range(4)]
                lprod = [sbt(es, n(f"lprod{i}"), [128, 64], F32) for i in range(2)]
                lsum = sbt(es, n("lsum"), [128, 2], F32)
                lexp = sbt(es, n("lexp"), [128, 2], F32)
                neglam = sbt(es, n("neglam"), [128, 1], F32)
                gsub = sbt(es, n("gsub"), [128, 1], F32)
                sinkraw = sbt(es, n("sinkraw"), [128, 2], F32)
                sinkexp = sbt(es, n("sinkexp"), [128, 2], F32)
                pss = [pst(es, n(f"pss{i}"), [128, 1024], F32) for i in range(2)]
                acc = [pst(es, n(f"acc{i}"), [128, 512], F32) for i in range(2)]
                zac = [pst(es, n(f"zac{i}"), [128, 512], F32) for i in range(2)]

                csl = slot(n("c"))
                jsl = [slot(n(f"j{i}")) for i in range(2)]
                obsl = [slot(n(f"ob{i}")) for i in range(2)]
                V = nc.vector

                csl.start(SP, diag[:], C["diaga"][:, :, :].rearrange("h s q -> s h q"))
                csl.start(SP, biasb[:], C["biasb"][:, :, :].rearrange("h s q -> s h q"))
                for i, nm in enumerate(("lam_q1", "lam_k1", "lam_q2", "lam_k2")):
                    csl.start(SP, lq[i][:], P[nm][l:l + 1, :].partition_broadcast(128))
                csl.start(SP, gsub[:], P["diff_subln_g"][l:l + 1, :].rearrange("o p -> p o"))
                for p in range(2):
                    csl.start(SP, sinkraw[0:64, p:p + 1], P["sink_logits"][l:l + 1, 2 * p:2 * p + 1].partition_broadcast(64))
                    t_c = csl.start(SP, sinkraw[64:128, p:p + 1],
                                    P["sink_logits"][l:l + 1, 2 * p + 1:2 * p + 2].partition_broadcast(64))
                DVE.wait(t_c)
                DVE.mark(V.tensor_tensor(out=lprod[0][:], in0=lq[0][:], in1=lq[1][:], op=ALU.mult))
                DVE.mark(V.tensor_tensor(out=lprod[1][:], in0=lq[2][:], in1=lq[3][:], op=ALU.mult))
                DVE.dep()
                DVE.mark(V.tensor_reduce(out=lsum[:, 0:1], in_=lprod[0][:], axis=AX.X, op=ALU.add))
                tl = DVE.mark(V.tensor_reduce(out=lsum[:, 1:2], in_=lprod[1][:], axis=AX.X, op=ALU.add))
                ACT.wait(tl, t_c)
                ACT.mark(nc.scalar.activation(out=lexp[:], in_=lsum[:], func=AF.Exp))
                te = ACT.mark(nc.scalar.activation(out=sinkexp[:], in_=sinkraw[:], func=AF.Exp))
                DVE.wait(te)
                DVE.mark(V.tensor_tensor(out=neglam[:], in0=lexp[:, 1:2], in1=lexp[:, 0:1], op=ALU.subtract))
                DVE.dep()
                DVE.mark(V.tensor_scalar(out=neglam[:], in0=neglam[:], scalar1=-lam_init, scalar2=None, op0=ALU.add))
                t_small = DVE.mark(V.tensor_scalar(out=gsub[:], in0=gsub[:], scalar1=1.0 - lam_init, scalar2=None, op0=ALU.mult))
                POOL.mark(nc.gpsimd.memset(Vb[0][:, :, 128:192], 0.0))
                t_vz = POOL.mark(nc.gpsimd.memset(Vb[1][:, :, 128:192], 0.0))

                jobs = [("A", 0), ("A", 1), ("A", 2), ("A", 3), ("C", 0), ("C", 1), ("B", 0), ("B", 1)]
                job_load_tok = [None] * len(jobs)
                job_done_tok = [None] * len(jobs)
                job_zero_tok = [None] * len(jobs)

                def load_job(ji):
                    kind, idx = jobs[ji]
                    sl = ji % 2
                    prev = job_done_tok[ji - 2] if ji >= 2 else None
                    SP.wait(prev)
                    q_, k_, v_ = Q[sl], Kb[sl], Vb[sl]
                    js = jsl[sl]
                    if kind == "A":
                        h = idx
                        for m in range(2):
                            js.start(SP, q_[0:4, m, :], C["alq"][h, :, :])
                            js.start(SP, q_[4:68, m, :], qkT[h * 128 + m * 64:h * 128 + (m + 1) * 64, :])
                            for v, nm in enumerate(("alka", "alkb")):
                                js.start(SP, k_[0:4, 2 * m + v, :], C[nm][h, :, :])
                                js.start(SP, k_[4:68, 2 * m + v, :], qkT[512 + h * 128 + m * 64:512 + h * 128 + (m + 1) * 64, :])
                        tok = js.start(SP, v_[:, :, 0:128], vscr[:, h * 128:(h + 1) * 128].rearrange("(c p) e -> p c e", p=128))
                    else:
                        p = idx
                        if kind == "C":
                            qsrc = cqkT[p * 128:(p + 1) * 128, :]
                            ksrc = cqkT[256 + p * 64:256 + (p + 1) * 64, :]
                            vcol = 640 + p * 64
                        else:
                            qsrc = qkT[1024 + p * 128:1024 + (p + 1) * 128, :]
                            ksrc = qkT[1280 + p * 64:1280 + (p + 1) * 64, :]
                            vcol = 512 + p * 64
                        if kind == "C":
                            POOL.wait(prev)
                            POOL.mark(nc.gpsimd.memset(v_[:, :, 64:128], 0.0))
                            POOL.mark(nc.gpsimd.memset(q_[64:128, 0, :], 0.0))
                            job_zero_tok[ji] = POOL.mark(nc.gpsimd.memset(q_[0:64, 1, :], 0.0))
                        js.start(SP, q_[0:64, 0, :], qsrc[0:64, :])
                        js.start(SP, q_[64:128, 1, :], qsrc[64:128, :])
                        js.start(SP, k_[0:64, 0, :], ksrc)
                        js.start(SP, k_[64:128, 0, :], ksrc)
                        vsrc = vscr[:, vcol:vcol + 64].rearrange("(c p) e -> p c e", p=128)
                        js.start(SP, v_[:, :, 0:64], vsrc)
                        tok = js.start(SP, v_[:, :, 192:256], vsrc)
                    job_load_tok[ji] = tok

                steps = []
                units = []

                def mm(out, lhsT, rhs, start, stop):
                    return nc.tensor.matmul(out, lhsT=lhsT, rhs=rhs, start=start, stop=stop)

                def add_unit(**kw):
                    kw["idx"] = len(units)
                    kw["steps"] = []
                    units.append(kw)
                    return kw

                def add_step(u, **kw):
                    kw["unit"] = u
                    kw["i"] = len(steps)
                    u["steps"].append(kw["i"])
                    steps.append(kw)

                def full_exp(pb, eb):
                    return et[eb][:], pss[pb][:]

                for ji, (kind, idx) in enumerate(jobs):
                    sl = ji % 2
                    if kind == "A":
                        h = idx
                        for j in range(NQ):
                            for m in range(2):
                                u = add_unit(kind="A", job=ji, h=h, j=j, m=m, zf=ones[:])

                                def S_tile(pb, off, c, sl=sl, h=h, j=j, m=m):
                                    q_, k_ = Q[sl], Kb[sl]
                                    if c > 4 * j + 3 or c < 4 * j:
                                        v = 0 if c > 4 * j + 3 else 1
                                        return mm(pss[pb][:, off:off + 512], k_[0:68, 2 * m + v, c * 128:(c + 1) * 128],
                                                  q_[0:68, m, j * 512:(j + 1) * 512], True, True)
                                    r = c - 4 * j
                                    ins = None
                                    for qb in range(4):
                                        cols = slice(off + qb * 128, off + (qb + 1) * 128)
                                        qc = slice(j * 512 + qb * 128, j * 512 + (qb + 1) * 128)
                                        v = 1 if qb > r else 0
                                        ins = mm(pss[pb][:, cols], k_[0:68, 2 * m + v, c * 128:(c + 1) * 128],
                                                 q_[0:68, m, qc], True, qb != r)
                                        if qb == r:
                                            ins = mm(pss[pb][:, cols], ident[:], diag[:, h, :], False, True)
                                    return ins
                                for s_ in range(NTT // 2):
                                    def S_fn(pb, s_=s_, S_tile=S_tile):
                                        S_tile(pb, 0, 2 * s_)
                                        return S_tile(pb, 512, 2 * s_ + 1)

                                    def PV_fn(eb, ub, first, last, sl=sl, s_=s_):
                                        c0, c1 = 2 * s_, 2 * s_ + 1
                                        mm(acc[ub][:], Vb[sl][:, c0, 0:128], et[eb][:, 0:512], first, False)
                                        mm(zac[ub][:], ones[:], et[eb][:, 0:512], first, False)
                                        return mm(acc[ub][:], Vb[sl][:, c1, 0:128], et[eb][:, 512:1024], False, last)
                                    add_step(u, S=S_fn, PV=PV_fn, exp=full_exp, dacc=lambda eb: et[eb][:, 512:1024])
                    elif kind == "C":
                        p = idx
                        for j in range(NQ):
                            u = add_unit(kind="C", job=ji, p=p, j=j, zf=onesp[:, 1, :])
                            for c in range(NTT):
                                def S_fn(pb, sl=sl, j=j, c=c):
                                    mm(pss[pb][:, 0:512], Kb[sl][:, 0, c * 128:(c + 1) * 128], Q[sl][:, 0, j * 512:(j + 1) * 512], True, True)
                                    return mm(pss[pb][:, 512:1024], Kb[sl][:, 0, c * 128:(c + 1) * 128],
                                              Q[sl][:, 1, j * 512:(j + 1) * 512], True, True)

                                def PV_fn(eb, ub, first, last, sl=sl, c=c):
                                    mm(acc[ub][:], Vb[sl][:, c, 0:128], et[eb][:, 0:512], first, False)
                                    mm(zac[ub][:], onesp[:, 0, :], et[eb][:, 0:512], first, False)
                                    return mm(acc[ub][:], Vb[sl][:, c, 128:256], et[eb][:, 512:1024], False, last)
                                add_step(u, S=S_fn, PV=PV_fn, exp=full_exp, dacc=lambda eb: et[eb][:, 512:1024])
                    else:
                        p = idx
                        for j in range(NQ):
                            u = add_unit(kind="B", job=ji, p=p, j=j, zf=None)
                            for qb in range(4 * j, 4 * j + 4):
                                rels = [r for r in (-1, 0, 1) if 0 <= qb + r < NTT]

                                def S_fn(pb, sl=sl, p=p, qb=qb, rels=rels):
                                    ins = None
                                    for hh in range(2):
                                        for r in rels:
                                            ri = r + 1
                                            cols = slice(hh * 512 + ri * 128, hh * 512 + (ri + 1) * 128)
                                            mm(pss[pb][:, cols], Kb[sl][:, 0, (qb + r) * 128:(qb + r + 1) * 128],
                                               Q[sl][:, hh, qb * 128:(qb + 1) * 128], True, False)
                                            ins = mm(pss[pb][:, cols], ident[:], biasb[:, 2 * p + hh, ri * 128:(ri + 1) * 128], False, True)
                                    return ins

                                def PV_fn(eb, ub, first, last, sl=sl, j=j, qb=qb, rels=rels):
                                    ins = None
                                    qc = slice((qb - 4 * j) * 128, (qb - 4 * j + 1) * 128)
                                    for hh in range(2):
                                        for r in rels:
                                            ri = r + 1
                                            cols = slice(hh * 512 + ri * 128, hh * 512 + (ri + 1) * 128)
                                            st_ = (hh == 0 and r == rels[0])
                                            sp_ = (hh == 1 and r == rels[-1])
                                            mm(acc[ub][:, qc], Vb[sl][:, qb + r, hh * 128:(hh + 1) * 128], et[eb][:, cols], st_, sp_)
                                            ins = mm(zac[ub][:, qc], onesp[:, hh, :], et[eb][:, cols], st_, sp_)
                                    return ins

                                def exp_fn(pb, eb, rels=rels):
                                    lo, hi = (rels[0] + 1) * 128, (rels[-1] + 2) * 128
                                    return (et[eb][:].rearrange("p (t c) -> p t c", t=2)[:, :, lo:hi],
                                            pss[pb][:].rearrange("p (t c) -> p t c", t=2)[:, :, lo:hi])
                                add_step(u, S=S_fn, PV=PV_fn, exp=exp_fn, dacc=None)

                NS_ = len(steps)
                tS = [None] * NS_
                tE = [None] * NS_
                tD = [None] * NS_
                accfree = [None, None]
                zsumfree = [None, None]
                zhfree = [None, None]
                zsplit = [None, None]
                sqbfree = [None, None]
                obsfree = [None, None]
                deferred = {}
                cnt = {"sq": 0, "obs": 0}
                first_step_of_job = {}
                for st_ in steps:
                    first_step_of_job.setdefault(st_["unit"]["job"], st_["i"])

                def defer(t, fn):
                    deferred.setdefault(min(t, NS_ - 1), []).append(fn)

                rzfree = [None]
                rsfree = [None]
                ofree = [None, None]

                def recip_on_act(src, bias=0.0):
                    ACT.mark(nc.scalar.activation(out=lz[:], in_=src, func=AF.Ln, bias=bias))
                    ACT.dep()
                    ACT.wait(rzfree[0])
                    return ACT.mark(nc.scalar.activation(out=rz[:], in_=lz[:], func=AF.Exp, scale=-1.0))

                def epilogue(u, tZ):
                    ub = u["idx"] % 2
                    j = u["j"]
                    js_ = slice(j * 512, (j + 1) * 512)
                    ACT.wait(tZ, t_small)
                    if u["kind"] == "A":
                        tr = recip_on_act(zac[ub][:])
                        DVE.wait(tr)
                        if u["m"] == 0:
                            t1 = DVE.mark(V.tensor_tensor(out=o1[:], in0=acc[ub][:], in1=rz[:], op=ALU.mult))
                            rzfree[0] = t1
                            accfree[ub] = [t1]
                            return
                        t2 = DVE.mark(V.tensor_tensor(out=o2[:], in0=acc[ub][:], in1=rz[:], op=ALU.mult))
                        rzfree[0] = t2
                        k = cnt["sq"] % 2
                        cnt["sq"] += 1
                        DVE.dep()
                        DVE.wait(ofree[k])
                        to = DVE.mark(V.scalar_tensor_tensor(out=obuf[k][:], in0=o2[:], scalar=neglam[:, 0:1], in1=o1[:],
                                                             op0=ALU.mult, op1=ALU.add))
                        POOL.wait(to, sqbfree[k])
                        tq = POOL.mark(nc.gpsimd.tensor_tensor(out=sqb[k][:], in0=obuf[k][:], in1=obuf[k][:], op=ALU.mult))
                        h = u["h"]
                        holder = {}

                        def ssq_pe(k=k, tq=tq, ub=ub, holder=holder):
                            PE.wait(tq)
                            holder["tm"] = PE.mark(mm(zac[ub][:], ones[:], sqb[k][:], True, True))
                            sqbfree[k] = holder["tm"]

                        def subln(k=k, ub=ub, h=h, js_=js_, t2=t2, holder=holder):
                            ACT.wait(holder["tm"])
                            tl = ACT.mark(nc.scalar.activation(out=ls[:], in_=zac[ub][:], func=AF.Ln, scale=1.0 / 128, bias=EPS))
                            accfree[ub] = [t2, tl]
                            ACT.dep()
                            ACT.wait(rsfree[0])
                            ts_ = ACT.mark(nc.scalar.activation(out=rsb[:], in_=ls[:], func=AF.Exp, scale=-0.5))
                            kk = cnt["obs"] % 2
                            cnt["obs"] += 1
                            DVE.wait(ts_, obsfree[kk])
                            tn = DVE.mark(V.scalar_tensor_tensor(out=obs[kk][:], in0=obuf[k][:], scalar=gsub[:, 0:1],
                                                                 in1=rsb[:], op0=ALU.mult, op1=ALU.mult))
                            rsfree[0] = tn
                            ofree[k] = tn
                            POOL.wait(tn)
                            obsfree[kk] = obsl[kk].start(POOL, mixT[h * 128:(h + 1) * 128, js_], obs[kk][:])
                        accfree[ub] = [t2]
                        defer(u["steps"][-1] + 6, ssq_pe)
                        defer(u["steps"][-1] + 8, subln)
                        return
                    p = u["p"]
                    if u["kind"] == "B":
                        tr = recip_on_act(zac[ub][:], bias=sinkexp[:, p:p + 1])
                        rowbase = 512 + p * 128
                    else:
                        tr = recip_on_act(zac[ub][:])
                        rowbase = 768 + p * 128
                    k = cnt["obs"] % 2
                    cnt["obs"] += 1
                    DVE.wait(tr, obsfree[k])
                    tb = DVE.mark(V.tensor_tensor(out=obs[k][:], in0=acc[ub][:], in1=rz[:], op=ALU.mult))
                    rzfree[0] = tb
                    accfree[ub] = [tb]
                    POOL.wait(tb)
                    obsfree[k] = obsl[k].start(POOL, mixT[rowbase:rowbase + 128, js_], obs[k][:])

                load_job(0)
                load_job(1)
                PE.wait(t_ident, t_ones, t_c, t_vz)
                for i in range(NS_ + 2):
                    if i < NS_:
                        st_ = steps[i]
                        u = st_["unit"]
                        ub = u["idx"] % 2
                        ji = u["job"]
                        first = (u["steps"][0] == i)
                        if first_step_of_job[ji] == i:
                            PE.wait(job_load_tok[ji], job_zero_tok[ji])
                        if i >= 2:
                            PE.wait(tE[i - 2])
                        if i >= NE:
                            PE.wait(*[t for t in tD[max(0, i - NE - 1):i - NE + 1] if t is not None][-1:])
                        tS[i] = PE.mark(st_["S"](i % 2))
                        ACT.wait(tS[i])
                        eo, ei = st_["exp"](i % 2, i % NE)
                        tE[i] = ACT.mark(nc.scalar.activation(out=eo, in_=ei, func=AF.Exp, scale=0.125))
                        if st_["dacc"] is not None:
                            DVE.wait(tE[i])
                            src = st_["dacc"](i % NE)
                            if first:
                                DVE.wait(zsumfree[ub])
                                tD[i] = DVE.mark(V.tensor_copy(out=zsum[ub][:], in_=src))
                            else:
                                DVE.dep()
                                tD[i] = DVE.mark(V.tensor_tensor(out=zsum[ub][:], in0=zsum[ub][:], in1=src, op=ALU.add))
                            if u["steps"][-1] == i:
                                DVE.dep()
                                DVE.wait(zhfree[ub])
                                DVE.mark(V.tensor_copy(out=zhi[ub][:], in_=zsum[ub][:]))
                                DVE.dep()
                                zsplit[ub] = DVE.mark(V.tensor_tensor(out=zlo[ub][:], in0=zsum[ub][:], in1=zhi[ub][:],
                                                                      op=ALU.subtract))
                    j_ = i - 2
                    if j_ >= 0:
                        st_ = steps[j_]
                        u = st_["unit"]
                        ub = u["idx"] % 2
                        PE.wait(tE[j_])
                        first = (u["steps"][0] == j_)
                        last = (u["steps"][-1] == j_)
                        if first:
                            PE.wait(*(accfree[ub] or []))
                        ins = st_["PV"](j_ % NE, ub, first, last)
                        if last:
                            tPV = PE.mark(ins)
                            job_done_tok[u["job"]] = tPV
                            if u["zf"] is not None:
                                hz = {}

                                def zfin(u=u, ub=ub, td=zsplit[ub], hz=hz):
                                    PE.wait(td)
                                    mm(zac[ub][:], u["zf"], zhi[ub][:], False, False)
                                    hz["tz"] = PE.mark(mm(zac[ub][:], u["zf"], zlo[ub][:], False, True))
                                    zhfree[ub] = hz["tz"]
                                defer(j_ + 1, zfin)
                                defer(j_ + 2, lambda u=u, hz=hz: epilogue(u, hz["tz"]))
                            else:
                                defer(j_ + 1, lambda u=u, tPV=tPV: epilogue(u, tPV))
                            ji = u["job"]
                            if j_ + 1 < NS_ and steps[j_ + 1]["unit"]["job"] != ji and ji + 2 < len(jobs):
                                load_job(ji + 2)
                        for fn in deferred.pop(j_, []):
                            fn()
                for k_ in sorted(deferred):
                    for fn in deferred[k_]:
                        fn()
                barrier()

        def post_norm_store(st, tt, pyA, pyB, tP, xt_tile, t_x):
            V = nc.vector
            ACT.wait(tP)
            ACT.mark(nc.scalar.activation(out=st["junk"][:, 0:512], in_=pyA, func=AF.Square, accum_out=st["ss2"][:, 2 * tt:2 * tt + 1]))
            a2 = ACT.mark(nc.scalar.activation(out=st["junk"][:, 512:1024], in_=pyB, func=AF.Square,
                                               accum_out=st["ss2"][:, 2 * tt + 1:2 * tt + 2]))
            DVE.wait(a2)
            d1 = DVE.mark(V.tensor_tensor(out=st["ss"][:, tt:tt + 1], in0=st["ss2"][:, 2 * tt:2 * tt + 1],
                                          in1=st["ss2"][:, 2 * tt + 1:2 * tt + 2], op=ALU.add))
            ACT.wait(d1)
            a3 = ACT.mark(nc.scalar.activation(out=st["sd"][:, tt:tt + 1], in_=st["ss"][:, tt:tt + 1], func=AF.Sqrt,
                                               scale=1.0 / D, bias=EPS))
            DVE.wait(a3)
            DVE.mark(V.reciprocal(out=st["rstd"][:, tt:tt + 1], in_=st["sd"][:, tt:tt + 1]))
            DVE.dep()
            k = st["yoi"] % len(st["yo"])
            st["yoi"] += 1
            yo = st["yo"][k]
            DVE.wait(st["yofree"][k], t_x, st["t_g"])
            DVE.mark(V.scalar_tensor_tensor(out=yo[:, 0:512], in0=pyA, scalar=st["rstd"][:, tt:tt + 1], in1=st["gpost"][:, 0:512],
                                            op0=ALU.mult, op1=ALU.mult))
            tfree = DVE.mark(V.scalar_tensor_tensor(out=yo[:, 512:1024], in0=pyB, scalar=st["rstd"][:, tt:tt + 1],
                                                    in1=st["gpost"][:, 512:1024], op0=ALU.mult, op1=ALU.mult))
            DVE.dep()
            tdone = DVE.mark(V.tensor_tensor(out=yo[:], in0=yo[:], in1=xt_tile[:], op=ALU.add))
            POOL.wait(tdone)
            st["yofree"][k] = st["yosl"][k].start(POOL, y[tt * 128:(tt + 1) * 128, :], yo[:])
            return tfree, tdone

        def load_wout(l, es):
            wout = sbt(es, f"wout_p4_{l}", [128, 8, D], BF16)
            wsl = slot(f"w_p4_{l}")
            for c in range(8):
                t_w = wsl.start(POOL, wout[:, c, :], P["w_out"][l, c * 128:(c + 1) * 128, :])
            return wout, t_w

        def phase4(l, wout, t_w):
            xsrc = x if l == 0 else y
            with ExitStack() as es:
                n = lambda s_: f"{s_}_p4_{l}"
                st = {
                    "gpost": sbt(es, n("gpost"), [128, D], F32),
                    "junk": sbt(es, n("junk"), [128, D], BF16),
                    "ss2": sbt(es, n("ss2"), [128, 2 * NTT], F32),
                    "ss": sbt(es, n("ss"), [128, NTT], F32),
                    "sd": sbt(es, n("sd"), [128, NTT], F32),
                    "rstd": sbt(es, n("rstd"), [128, NTT], F32),
                    "yo": [sbt(es, n(f"yo{i}"), [128, D], F32) for i in range(2)],
                    "yofree": [None, None], "yoi": 0,
                    "yosl": [slot(n(f"yo{i}")) for i in range(2)],
                }
                mixt = [sbt(es, n(f"mixt{i}"), [128, 8, 512], BF16) for i in range(2)]
                xt = [sbt(es, n(f"xt{i}"), [128, D], F32) for i in range(3)]
                py = [pst(es, n(f"py{i}"), [128, 512], F32) for i in range(4)]
                gsl = slot(n("g"))
                msl = [slot(n(f"m{i}")) for i in range(2)]
                xs = [slot(n(f"x{i}")) for i in range(3)]
                st["t_g"] = gsl.start(SP, st["gpost"][:], P["g_post_mix"][l:l + 1, :].partition_broadcast(128))
                mfree = [None, None]
                xfree = [None] * 3
                pyfree = [None, None]
                for grp in range(NQ):
                    mk = grp % 2
                    SP.wait(mfree[mk])
                    t_m = msl[mk].start(SP, mixt[mk][:], mixT[:, grp * 512:(grp + 1) * 512].rearrange("(c p) t -> p c t", p=128))
                    for ti in range(4):
                        tt = grp * 4 + ti
                        sl = tt % 3
                        SP.wait(xfree[sl])
                        t_x = xs[sl].start(SP, xt[sl][:], xsrc[tt * 128:(tt + 1) * 128, :])
                        pk = tt % 2
                        PE.wait(t_m, t_w, pyfree[pk])
                        for nh in range(2):
                            for c in range(8):
                                ins = nc.tensor.matmul(py[2 * pk + nh][:], lhsT=mixt[mk][:, c, ti * 128:(ti + 1) * 128],
                                                       rhs=wout[:, c, nh * 512:(nh + 1) * 512], start=(c == 0), stop=(c == 7))
                        tP = PE.mark(ins)
                        if ti == 3:
                            mfree[mk] = tP
                        pyfree[pk], xfree[sl] = post_norm_store(st, tt, py[2 * pk][:], py[2 * pk + 1][:], tP, xt[sl], t_x)
                barrier()

        def mlp_weights(l, es):
            n = lambda s_: f"{s_}_p5_{l}"
            w1 = sbt(es, n("w1"), [128, 8, DFF], BF16)
            w2 = sbt(es, n("w2"), [128, 32, D], BF16)
            hold = {}
            w1sl = [slot(n(f"w1b{i}")) for i in range(8)]
            w2sl = [slot(n(f"w2b{i}")) for i in range(8)]

            def issue():
                pass

            def issue2():
                hold["t_w1"] = [w1sl[i].start(POOL, w1[:, :, i * 512:(i + 1) * 512],
                                              P["w_mlp_in"][l, :, i * 512:(i + 1) * 512].rearrange("(c p) n -> p c n", p=128))
                                for i in range(8)]
                hold["t_w2"] = [w2sl[i].start(POOL, w2[:, 4 * i:4 * i + 4, :],
                                              P["w_mlp_out"][l, i * 512:(i + 1) * 512, :].rearrange("(f p) n -> p f n", p=128))
                                for i in range(8)]
            hold["issue2"] = issue2
            return w1, w2, issue, hold

        def phase5(l, w1, w2, hold_w):
            hold_w["issue2"]()
            t_w1b, t_w2b = hold_w["t_w1"], hold_w["t_w2"]
            with ExitStack() as es:
                n = lambda s_: f"{s_}_p5_{l}"
                gpre = sbt(es, n("gpre"), [128, D], F32)
                st = {
                    "gpost": sbt(es, n("gpost"), [128, D], F32),
                    "junk": sbt(es, n("junk"), [128, D], BF16),
                    "ss2": sbt(es, n("ss2"), [128, 2 * NTT], F32),
                    "ss": sbt(es, n("ss"), [128, NTT], F32),
                    "sd": sbt(es, n("sd"), [128, NTT], F32),
                    "rstd": sbt(es, n("rstd"), [128, NTT], F32),
                    "yo": [sbt(es, n(f"yo{i}"), [128, D], F32) for i in range(2)],
                    "yofree": [None, None], "yoi": 0,
                    "yosl": [slot(n(f"yo{i}")) for i in range(2)],
                }
                ssp = sbt(es, n("ssp"), [128, NTT], F32)
                sdp = sbt(es, n("sdp"), [128, NTT], F32)
                rsp = sbt(es, n("rsp"), [128, NTT], F32)
                uT = sbt(es, n("uT"), [128, 32, 512], BF16)
                h2T = sbt(es, n("h2T"), [128, 8, 512], BF16)
                hb = [sbt(es, n(f"hb{i}"), [128, D], BF16) for i in range(2)]
                xt = [sbt(es, n(f"xt{i}"), [128, D], F32) for i in range(3)]
                ptr = [pst(es, n(f"ptr{i}"), [128, D], BF16) for i in range(2)]
                pu = [pst(es, n(f"pu{i}"), [128, 512], F32) for i in range(2)]
                py = [pst(es, n(f"py{i}"), [128, 512], F32) for i in range(4)]
                gsl = slot(n("g"))
                xs = [slot(n(f"x{i}")) for i in range(3)]
                V = nc.vector
                gsl.start(SP, gpre[:], P["g_pre_mlp"][l:l + 1, :].partition_broadcast(128))
                st["t_g"] = gsl.start(SP, st["gpost"][:], P["g_post_mlp"][l:l + 1, :].partition_broadcast(128))
                xfree = [None] * 3
                hbfree = [None, None]
                ptrfree = [None, None]
                pufree = [None, None]
                pyfree = [None, None]
                h2free = [None]
                uTfree = [None]
                xi = {"n": 0}

                def load_x(tt):
                    sl = xi["n"] % 3
                    xi["n"] += 1
                    SP.wait(xfree[sl])
                    return sl, xs[sl].start(SP, xt[sl][:], y[tt * 128:(tt + 1) * 128, :])

                hb_tok = {}

                def prep0(tt):
                    sl, t_x = load_x(tt)
                    ACT.wait(t_x, hbfree[tt % 2])
                    t1 = ACT.mark(nc.scalar.activation(out=hb[tt % 2][:], in_=xt[sl][:], func=AF.Square, accum_out=ssp[:, tt:tt + 1]))
                    ACT.wait(t1)
                    t2 = ACT.mark(nc.scalar.activation(out=sdp[:, tt:tt + 1], in_=ssp[:, tt:tt + 1], func=AF.Sqrt, scale=1.0 / D, bias=EPS))
                    DVE.wait(t2)
                    DVE.mark(V.reciprocal(out=rsp[:, tt:tt + 1], in_=sdp[:, tt:tt + 1]))
                    DVE.dep()
                    DVE.wait(hbfree[tt % 2], st["t_g"], t_x)
                    t4 = DVE.mark(V.scalar_tensor_tensor(out=hb[tt % 2][:], in0=xt[sl][:], scalar=rsp[:, tt:tt + 1], in1=gpre[:],
                                                         op0=ALU.mult, op1=ALU.mult))
                    xfree[sl] = t4
                    hb_tok[tt] = t4

                def trans0(tt):
                    ti = tt % 4
                    PE.wait(hb_tok[tt], ptrfree[tt % 2], t_ident)
                    for c in range(8):
                        ins = nc.tensor.transpose(ptr[tt % 2][:, c * 128:(c + 1) * 128], hb[tt % 2][:, c * 128:(c + 1) * 128], ident[:])
                    t5 = PE.mark(ins)
                    hbfree[tt % 2] = t5
                    ACT.wait(t5, h2free[0])
                    t6 = ACT.mark(nc.scalar.activation(out=h2T[:, :, ti * 128:(ti + 1) * 128],
                                                       in_=ptr[tt % 2][:].rearrange("p (c t) -> p c t", c=8), func=AF.Copy))
                    ptrfree[tt % 2] = t6
                    return t6

                def stage0(grp):
                    t6 = None
                    for ti in range(4):
                        prep0(grp * 4 + ti)
                        t6 = trans0(grp * 4 + ti)
                    return t6

                def stage1(grp, t_h2):
                    PE.wait(t_h2)
                    last = None
                    for f in range(32):
                        b = f % 2
                        PE.wait(pufree[b], t_w1b[f // 4])
                        if f == 0:
                            PE.wait(uTfree[0])
                        for c in range(8):
                            ins = nc.tensor.matmul(pu[b][:], lhsT=w1[:, c, f * 128:(f + 1) * 128], rhs=h2T[:, c, :],
                                                   start=(c == 0), stop=(c == 7))
                        tP = PE.mark(ins)
                        ACT.wait(tP, uTfree[0])
                        ta = ACT.mark(nc.scalar.activation(out=uT[:, f, :], in_=pu[b][:], func=AF.Square))
                        DVE.wait(ta)
                        last = DVE.mark(V.scalar_tensor_tensor(out=uT[:, f, :], in0=pu[b][:], scalar=0.0, in1=uT[:, f, :],
                                                               op0=ALU.is_gt, op1=ALU.mult))
                        pufree[b] = last
                    h2free[0] = tP
                    return last

                def stage2(grp, t_u, tiles=(0, 1, 2, 3)):
                    PE.wait(t_u)
                    for ti in tiles:
                        tt = grp * 4 + ti
                        sl, t_x = load_x(tt)
                        pk = tt % 2
                        PE.wait(pyfree[pk])
                        for nh in range(2):
                            for f in range(32):
                                if f % 4 == 0:
                                    PE.wait(t_w2b[f // 4])
                                ins = nc.tensor.matmul(py[2 * pk + nh][:], lhsT=uT[:, f, ti * 128:(ti + 1) * 128],
                                                       rhs=w2[:, f, nh * 512:(nh + 1) * 512], start=(f == 0), stop=(f == 31))
                        tP = PE.mark(ins)
                        if ti == 3:
                            uTfree[0] = tP
                        pyfree[pk], xfree[sl] = post_norm_store(st, tt, py[2 * pk][:], py[2 * pk + 1][:], tP, xt[sl], t_x)

                t_h = stage0(0)
                for grp in range(NQ):
                    t_u = stage1(grp, t_h)
                    if grp + 1 < NQ:
                        g1 = 4 * (grp + 1)
                        prep0(g1)
                        prep0(g1 + 1)
                        stage2(grp, t_u, (0,))
                        trans0(g1)
                        trans0(g1 + 1)
                        prep0(g1 + 2)
                        stage2(grp, t_u, (1,))
                        trans0(g1 + 2)
                        prep0(g1 + 3)
                        stage2(grp, t_u, (2,))
                        t_h = trans0(g1 + 3)
                        stage2(grp, t_u, (3,))
                    else:
                        stage2(grp, t_u)
                barrier()

        for l in layers:
            if "p1" in phases:
                with nc.named_scope(f"L{l}_p1"):
                    phase1(l)
            with ExitStack() as es34:
                if "p4" in phases:
                    wout, t_wout = load_wout(l, es34)
                if "p3" in phases:
                    with nc.named_scope(f"L{l}_p3"):
                        phase3(l)
                if "p4" in phases:
                    with nc.named_scope(f"L{l}_p4"):
                        phase4(l, wout, t_wout)
            with ExitStack() as es45:
                if "p5" in phases:
                    w1, w2, issue_w, hold_w = mlp_weights(l, es45)
                    with nc.named_scope(f"L{l}_p5"):
                        phase5(l, w1, w2, hold_w)
        barrier()
    return nc


_NC_CACHE = {}


def _in_maps(inputs, consts):
    maps = []
    shared = {k: np.ascontiguousarray(np.asarray(inputs[k], dtype=np.float32)) for k in PARAM_SPECS}
    xs = np.asarray(inputs["x"], dtype=np.float32)
    for b in range(8):
        m = {"x": np.ascontiguousarray(xs[b])}
        m.update(shared)
        m.update(consts)
        maps.append(m)
    return maps


def kernel(**inputs):
    if "nc" not in _NC_CACHE:
        _NC_CACHE["nc"] = build()
        _NC_CACHE["consts"] = _consts()
    nc = _NC_CACHE["nc"]
    maps = _in_maps(inputs, _NC_CACHE["consts"])
    res = run_bass_kernel_spmd(nc, maps, core_ids=list(range(8)))
    return np.stack([np.asarray(r["y"], dtype=np.float32) for r in res.results], axis=0)
```

```python
import math
import numpy as np
import ml_dtypes
from contextlib import ExitStack
import concourse.bass as bass
import concourse.mybir as mybir
from concourse.bass_utils import run_bass_kernel_spmd

F32 = mybir.dt.float32
BF16 = mybir.dt.bfloat16
AF = mybir.ActivationFunctionType
ALU = mybir.AluOpType
AX = mybir.AxisListType

S = 4096
D = 1024
NTT = 32
NQ = 8
DFF = 4096
EPS = 1e-6
DEPTH = 2
NEG = -60000.0


def _consts():
    bf = ml_dtypes.bfloat16
    c = {}
    c["ident"] = np.eye(128, dtype=np.float32).astype(bf)
    pos = np.arange(S)
    lo = (pos % 256).astype(np.float32)
    hi = (pos - pos % 256).astype(np.float32)
    one = np.ones(S, np.float32)
    alq = np.zeros((4, 4, S), np.float32)
    alk = np.zeros((4, 4, S), np.float32)
    diag = np.zeros((4, 128, 128), np.float32)
    ab = np.abs(np.arange(128)[:, None] - np.arange(128)[None, :]).astype(np.float32)
    for h in range(4):
        m8 = 8.0 * 2.0 ** (-(2 * h + 1))
        alq[h] = np.stack([m8 * hi, one, m8 * lo, one])
        alk[h] = np.stack([one, -m8 * hi, one, -m8 * lo])
        diag[h] = -2.0 * m8 * np.maximum(np.arange(128)[None, :] - np.arange(128)[:, None], 0)
    c["alq"] = alq.astype(bf)
    c["alka"] = alk.astype(bf)
    c["alkb"] = (-alk).astype(bf)
    c["diaga"] = diag.astype(bf)
    bb = np.zeros((4, 128, 384), np.float32)
    b_ = np.arange(128)[:, None]
    a_ = np.arange(128)[None, :]
    for h in range(4):
        m8 = 8.0 * 2.0 ** (-(2 * h + 2))
        for ri, rel in enumerate((-1, 0, 1)):
            delta = rel * 128 + b_ - a_
            v = np.where(np.abs(delta) <= 128, -m8 * np.abs(delta), NEG)
            bb[h][:, ri * 128:(ri + 1) * 128] = v
    c["biasb"] = bb.astype(bf)
    row = (pos // 64).astype(np.float32)
    col = (pos % 64).astype(np.float32)
    freqs = (np.float32(10000.0) ** (-np.arange(0, 32, 2, dtype=np.float32) / np.float32(32))).astype(np.float32)
    ar = (row[:, None] * freqs[None]).astype(np.float32)
    ac = (col[:, None] * freqs[None]).astype(np.float32)
    rope = np.concatenate([np.cos(ar), np.sin(ar), np.cos(ac), np.sin(ac)], axis=1).astype(np.float32)
    c["rope"] = np.ascontiguousarray(rope.reshape(NTT, 128, 64).transpose(1, 0, 2))
    return c


CONST_SPECS = {
    "ident": ([128, 128], BF16), "alq": ([4, 4, S], BF16), "alka": ([4, 4, S], BF16), "alkb": ([4, 4, S], BF16),
    "diaga": ([4, 128, 128], BF16), "biasb": ([4, 128, 384], BF16), "rope": ([128, NTT, 64], F32),
}
PARAM_SPECS = {
    "w_in": [DEPTH, D, 2560], "w_out": [DEPTH, D, D], "g_pre_mix": [DEPTH, D], "g_post_mix": [DEPTH, D],
    "lam_q1": [DEPTH, 64], "lam_k1": [DEPTH, 64], "lam_q2": [DEPTH, 64], "lam_k2": [DEPTH, 64],
    "diff_subln_g": [DEPTH, 128], "sink_logits": [DEPTH, 4], "c_q_norm": [DEPTH, 64], "c_k_norm": [DEPTH, 64],
    "g_pre_mlp": [DEPTH, D], "g_post_mlp": [DEPTH, D], "w_mlp_in": [DEPTH, D, DFF], "w_mlp_out": [DEPTH, DFF, D],
}


class Eng:
    def __init__(self, nc, es, name, obj):
        self.name, self.o = name, obj
        self.sem = es.enter_context(nc.semaphore("pg_" + name))
        self.cnt = 0
        self.waited = {}
        self.last = None

    def wait(self, *toks):
        for tok in toks:
            if tok is None:
                continue
            sem, val = tok
            if self.waited.get(id(sem), 0) >= val:
                continue
            self.o.wait_ge(sem, val)
            self.waited[id(sem)] = val

    def mark(self, ins):
        self.cnt += 1
        ins.then_inc(self.sem, 1)
        self.last = (self.sem, self.cnt)
        return self.last

    def dep(self):
        if self.last is not None:
            self.wait(self.last)


class DmaSlot:
    def __init__(self, nc, es, name):
        self.sem = es.enter_context(nc.semaphore("dq_" + name))
        self.cnt = 0

    def start(self, eng, out, in_):
        eng.o.dma_start(out=out, in_=in_).then_inc(self.sem, 16)
        self.cnt += 16
        return (self.sem, self.cnt)

    def tok(self):
        return (self.sem, self.cnt) if self.cnt else None


def build(debug=False, layers=(0, 1), phases=("p1", "p3", "p4", "p5")):
    nc = bass.Bass("TRN2", target_bir_lowering=False)

    def din(name, shape, dt=F32):
        return nc.dram_tensor(name, list(shape), dt, kind="ExternalInput").ap()

    x = din("x", [S, D])
    P = {k: din(k, v) for k, v in PARAM_SPECS.items()}
    C = {k: din(k, v[0], v[1]) for k, v in CONST_SPECS.items()}
    y = nc.dram_tensor("y", [S, D], F32, kind="ExternalOutput").ap()
    skind = "ExternalOutput" if debug else "Internal"
    qkT = nc.dram_tensor("qkT", [1408, S], BF16, kind=skind).ap()
    vscr = nc.dram_tensor("vscr", [S, 768], BF16, kind=skind).ap()
    cqkT = nc.dram_tensor("cqkT", [384, S], BF16, kind=skind).ap()
    mixT = nc.dram_tensor("mixT", [1024, S], BF16, kind=skind).ap()

    with ExitStack() as top:
        PE = Eng(nc, top, "pe", nc.tensor)
        ACT = Eng(nc, top, "act", nc.scalar)
        DVE = Eng(nc, top, "dve", nc.vector)
        POOL = Eng(nc, top, "pool", nc.gpsimd)
        SP = Eng(nc, top, "sp", nc.sync)
        ENGS = [PE, ACT, DVE, POOL, SP]
        slots = []

        slot_cache = {}

        def slot(name):
            key = name.rsplit("_", 1)[0] if name[-1].isdigit() and "_p" in name else name
            if key not in slot_cache:
                slot_cache[key] = DmaSlot(nc, top, key)
                slots.append(slot_cache[key])
            return slot_cache[key]

        def sbt(es, name, shape, dt):
            return es.enter_context(nc.sbuf_tensor("sb_" + name, list(shape), dt))

        def pst(es, name, shape, dt):
            return es.enter_context(nc.psum_tensor("ps_" + name, list(shape), dt))

        ident = sbt(top, "ident", [128, 128], BF16)
        ones = sbt(top, "ones", [128, 128], BF16)
        onesp = sbt(top, "onesp", [128, 2, 128], BF16)
        cs = slot("const")
        t_ident = cs.start(SP, ident[:], C["ident"][:, :])
        DVE.mark(nc.vector.memset(ones[:], 1.0))
        DVE.mark(nc.vector.memset(onesp[:], 0.0))
        DVE.dep()
        DVE.mark(nc.vector.memset(onesp[:, 0, 0:64], 1.0))
        t_ones = DVE.mark(nc.vector.memset(onesp[:, 1, 64:128], 1.0))

        def barrier():
            toks = [e.last for e in (PE, ACT, DVE, POOL)] + [s.tok() for s in slots]
            for e in ENGS:
                e.wait(*toks)

        def phase1(l):
            xsrc = x if l == 0 else y
            with ExitStack() as es:
                n = lambda s_: f"{s_}_p1_{l}"
                hT = sbt(es, n("hT"), [128, 8, S], BF16)
                win = sbt(es, n("win"), [128, 8, 2560], BF16)
                gbc = sbt(es, n("gbc"), [128, D], F32)
                gqk = sbt(es, n("gqk"), [128, 384], F32)
                rope = sbt(es, n("rope"), [128, NTT, 64], F32)
                xt = [sbt(es, n(f"xt{i}"), [128, D], F32) for i in range(3)]
                hb = [sbt(es, n(f"hb{i}"), [128, D], BF16) for i in range(4)]
                junk = sbt(es, n("junk"), [128, D], F32)
                ss = sbt(es, n("ss"), [128, NTT], F32)
                sd = sbt(es, n("sd"), [128, NTT], F32)
                rstd = sbt(es, n("rstd"), [128, NTT], F32)
                stg = [sbt(es, n(f"stg{i}"), [128, 512], BF16) for i in range(4)]
                vst = [sbt(es, n(f"vst{i}"), [128, 768], BF16) for i in range(2)]
                cqf = [sbt(es, n(f"cqf{i}"), [128, 384], F32) for i in range(2)]
                csq = sbt(es, n("csq"), [128, 384], F32)
                css = sbt(es, n("css"), [128, NTT, 6], F32)
                csd = sbt(es, n("csd"), [128, NTT, 6], F32)
                crr = sbt(es, n("crr"), [128, NTT, 6], F32)
                cqns = [sbt(es, n(f"cqn{i}"), [128, 384], F32) for i in range(2)]
                tA = sbt(es, n("tA"), [128, 192], F32)
                tB = sbt(es, n("tB"), [128, 192], F32)
                crb = [sbt(es, n(f"crb{i}"), [128, 384], BF16) for i in range(2)]
                cst = [sbt(es, n(f"cst{i}"), [128, 3, 128], BF16) for i in range(2)]
                ptr = [pst(es, n(f"ptr{i}"), [128, D], BF16) for i in range(2)]
                pp = [pst(es, n(f"pp{i}"), [128, 512], F32) for i in range(4)]
                ptc = pst(es, n("ptc"), [128, 512], BF16)

                wsl = slot(n("w"))
                gsl = slot(n("g"))
                xs = [slot(n(f"x{i}")) for i in range(3)]
                sts = [slot(n(f"st{i}")) for i in range(4)]
                vss = [slot(n(f"vs{i}")) for i in range(2)]
                css_ = [slot(n(f"cs{i}")) for i in range(2)]

                wsl2 = slot(n("w2"))

                def wblk(sl_, a, b):
                    return sl_.start(POOL, win[:, :, a:b], P["w_in"][l, :, a:b].rearrange("(c p) n -> p c n", p=128))
                wblk(wsl, 1024, 1536)
                t_win_tm = wblk(wsl, 1920, 2560)
                wblk(wsl2, 0, 1024)
                t_win_fm = wblk(wsl2, 1536, 1920)
                gsl.start(SP, gbc[:], P["g_pre_mix"][l:l + 1, :].partition_broadcast(128))
                for hh in range(4):
                    gsl.start(SP, gqk[:, hh * 64:(hh + 1) * 64], P["c_q_norm"][l:l + 1, :].partition_broadcast(128))
                for hh in range(2):
                    gsl.start(SP, gqk[:, 256 + hh * 64:256 + (hh + 1) * 64],
                              P["c_k_norm"][l:l + 1, :].partition_broadcast(128))
                t_g = gsl.start(SP, rope[:], C["rope"][:, :, :])

                xfree = [None] * 3
                hbfree = [None] * 4
                hb_tok = {}
                ptrfree = [None] * 2
                ppfree = [None] * 4
                stgfree = [None] * 4
                vstfree = [None] * 2
                cstfree = [None] * 2
                crbfree = [None] * 2
                cqffree = [None] * 2
                ptcfree = [None]
                hT_tok = [None] * NTT
                st = {"pp": 0, "stg": 0, "ev": 0}

                def next_pp():
                    b = st["pp"] % 4
                    st["pp"] += 1
                    PE.wait(ppfree[b])
                    return b

                def evac_eng():
                    return ACT

                def copy_on(e, out, in_):
                    if e is ACT:
                        return ACT.mark(nc.scalar.activation(out=out, in_=in_, func=AF.Copy))
                    return DVE.mark(nc.vector.tensor_copy(out=out, in_=in_))

                def stepA(tt):
                    sl = tt % 3
                    SP.wait(xfree[sl])
                    t_x = xs[sl].start(SP, xt[sl][:], xsrc[tt * 128:(tt + 1) * 128, :])
                    ACT.wait(t_x)
                    t1 = ACT.mark(nc.scalar.activation(out=junk[:], in_=xt[sl][:], func=AF.Square,
                                                       accum_out=ss[:, tt:tt + 1]))
                    ACT.wait(t1)
                    t2 = ACT.mark(nc.scalar.activation(out=sd[:, tt:tt + 1], in_=ss[:, tt:tt + 1], func=AF.Sqrt,
                                                       scale=1.0 / D, bias=EPS))
                    DVE.wait(t2)
                    t3 = DVE.mark(nc.vector.reciprocal(out=rstd[:, tt:tt + 1], in_=sd[:, tt:tt + 1]))
                    DVE.wait(t3, hbfree[tt % 4], t_g, t_x)
                    t4 = DVE.mark(nc.vector.scalar_tensor_tensor(out=hb[tt % 4][:], in0=xt[sl][:],
                                                                 scalar=rstd[:, tt:tt + 1], in1=gbc[:],
                                                                 op0=ALU.mult, op1=ALU.mult))
                    xfree[sl] = t4
                    hb_tok[tt] = t4

                def transA(tt):
                    PE.wait(hb_tok[tt], ptrfree[tt % 2], t_ident)
                    for c in range(8):
                        ins = nc.tensor.transpose(ptr[tt % 2][:, c * 128:(c + 1) * 128],
                                                  hb[tt % 4][:, c * 128:(c + 1) * 128], ident[:])
                    t5 = PE.mark(ins)
                    hbfree[tt % 4] = t5
                    ACT.wait(t5)
                    t6 = ACT.mark(nc.scalar.activation(out=hT[:, :, tt * 128:(tt + 1) * 128],
                                                       in_=ptr[tt % 2][:].rearrange("p (c t) -> p c t", c=8),
                                                       func=AF.Copy))
                    ptrfree[tt % 2] = t6
                    hT_tok[tt] = t6

                def fm_cols(g):
                    if g < 4:
                        return g * 128
                    if g < 8:
                        return 512 + (g - 4) * 128
                    if g < 10:
                        return 1536 + (g - 8) * 128
                    return 1792

                def proj_fm(j, groups):
                    PE.wait(hT_tok[4 * j + 3], t_win_fm)
                    for g in groups:
                        b = next_pp()
                        c0 = fm_cols(g)
                        for c in range(8):
                            ins = nc.tensor.matmul(pp[b][:], lhsT=win[:, c, c0:c0 + 128],
                                                   rhs=hT[:, c, j * 512:(j + 1) * 512], start=(c == 0), stop=(c == 7))
                        tP = PE.mark(ins)
                        ev = evac_eng()
                        s_ = st["stg"] % 4
                        st["stg"] += 1
                        ev.wait(tP, stgfree[s_])
                        tE = copy_on(ev, stg[s_][:], pp[b][:])
                        ppfree[b] = tE
                        POOL.wait(tE)
                        stgfree[s_] = sts[s_].start(POOL, qkT[g * 128:(g + 1) * 128, j * 512:(j + 1) * 512], stg[s_][:])

                def proj_tm(tt):
                    PE.wait(hT_tok[tt], t_win_tm)
                    lt = lambda c: hT[:, c, tt * 128:(tt + 1) * 128]
                    b1 = next_pp()
                    for c in range(8):
                        ins = nc.tensor.matmul(pp[b1][:], lhsT=lt(c), rhs=win[:, c, 1024:1536], start=(c == 0), stop=(c == 7))
                    tP1 = PE.mark(ins)
                    b2 = next_pp()
                    for c in range(8):
                        nc.tensor.matmul(pp[b2][:, 0:128], lhsT=lt(c), rhs=win[:, c, 1920:2048], start=(c == 0), stop=(c == 7))
                    for c in range(8):
                        ins = nc.tensor.matmul(pp[b2][:, 128:256], lhsT=lt(c), rhs=win[:, c, 2432:2560], start=(c == 0), stop=(c == 7))
                    tP2 = PE.mark(ins)
                    b3 = next_pp()
                    for c in range(8):
                        ins = nc.tensor.matmul(pp[b3][:, 0:384], lhsT=lt(c), rhs=win[:, c, 2048:2432], start=(c == 0), stop=(c == 7))
                    tP3 = PE.mark(ins)
                    vs = tt % 2
                    ACT.wait(tP1, vstfree[vs])
                    tE1 = ACT.mark(nc.scalar.activation(out=vst[vs][:, 0:512], in_=pp[b1][:], func=AF.Copy))
                    ppfree[b1] = tE1
                    ACT.wait(tP2)
                    tE2 = ACT.mark(nc.scalar.activation(out=vst[vs][:, 512:768], in_=pp[b2][:, 0:256], func=AF.Copy))
                    ppfree[b2] = tE2
                    POOL.wait(tE1, tE2)
                    vstfree[vs] = vss[vs].start(POOL, vscr[tt * 128:(tt + 1) * 128, :], vst[vs][:])
                    cf = cqf[tt % 2]
                    cqn = cqns[tt % 2]
                    cqnfree = [cqn_done[tt % 2]]
                    ACT.wait(tP3, cqffree[tt % 2])
                    tc0 = ACT.mark(nc.scalar.activation(out=cf[:], in_=pp[b3][:, 0:384], func=AF.Copy))
                    ppfree[b3] = tc0
                    V = nc.vector
                    DVE.wait(tc0)
                    DVE.dep()
                    DVE.mark(V.tensor_tensor(out=csq[:], in0=cf[:], in1=cf[:], op=ALU.mult))
                    DVE.dep()
                    tc2 = DVE.mark(V.tensor_reduce(out=css[:, tt, :], in_=csq[:].rearrange("p (h d) -> p h d", d=64),
                                                   axis=AX.X, op=ALU.add))
                    ACT.wait(tc2)
                    tc3 = ACT.mark(nc.scalar.activation(out=csd[:, tt, :], in_=css[:, tt, :], func=AF.Sqrt,
                                                       scale=1.0 / 64, bias=EPS))
                    DVE.wait(tc3)
                    DVE.mark(V.reciprocal(out=crr[:, tt, :], in_=csd[:, tt, :]))
                    DVE.dep()
                    rb = crr[:, tt, :].rearrange("p (h o) -> p h o", o=1).broadcast_to([128, 6, 64])
                    DVE.wait(cqnfree[0])
                    DVE.mark(V.tensor_tensor(out=cqn[:].rearrange("p (h d) -> p h d", d=64),
                                             in0=cf[:].rearrange("p (h d) -> p h d", d=64), in1=rb, op=ALU.mult))
                    DVE.dep()
                    DVE.wait(t_g)
                    tcn = DVE.mark(V.tensor_tensor(out=cqn[:], in0=cqn[:], in1=gqk[:], op=ALU.mult))
                    cqffree[tt % 2] = tcn
                    X = cqn[:].rearrange("p (h f t d) -> p h f t d", h=6, f=2, t=2, d=16)
                    x1, x2 = X[:, :, :, 0, :], X[:, :, :, 1, :]
                    R = rope[:, tt, :].rearrange("p (o f k d) -> p o f k d", o=1, f=2, k=2, d=16)
                    cosb = R[:, :, :, 0, :].broadcast_to([128, 6, 2, 16])
                    sinb = R[:, :, :, 1, :].broadcast_to([128, 6, 2, 16])
                    cb = crb[tt % 2]
                    Y = cb[:].rearrange("p (h f t d) -> p h f t d", h=6, f=2, t=2, d=16)
                    y1, y2 = Y[:, :, :, 0, :], Y[:, :, :, 1, :]
                    TA = tA[:].rearrange("p (h f d) -> p h f d", h=6, f=2, d=16)
                    TB = tB[:].rearrange("p (h f d) -> p h f d", h=6, f=2, d=16)
                    G = nc.gpsimd
                    POOL.wait(tcn, t_g, crbfree[tt % 2], cqnfree[0])
                    POOL.dep()
                    POOL.mark(G.tensor_tensor(out=TA, in0=x1, in1=cosb, op=ALU.mult))
                    POOL.mark(G.tensor_tensor(out=TB, in0=x2, in1=sinb, op=ALU.mult))
                    POOL.dep()
                    POOL.mark(G.tensor_tensor(out=y1, in0=TA, in1=TB, op=ALU.subtract))
                    POOL.dep()
                    POOL.mark(G.tensor_tensor(out=TA, in0=x2, in1=cosb, op=ALU.mult))
                    POOL.mark(G.tensor_tensor(out=TB, in0=x1, in1=sinb, op=ALU.mult))
                    POOL.dep()
                    tcr = POOL.mark(G.tensor_tensor(out=y2, in0=TA, in1=TB, op=ALU.add))
                    cqn_done[tt % 2] = tcr
                    pend_c.append((tt, tcr, cb))

                def proj_tm_finish():
                    tt, tcr, cb = pend_c.pop(0)
                    PE.wait(tcr, ptcfree[0])
                    for g in range(3):
                        ins = nc.tensor.transpose(ptc[:, g * 128:(g + 1) * 128], cb[:, g * 128:(g + 1) * 128], ident[:])
                    tct = PE.mark(ins)
                    crbfree[tt % 2] = tct
                    cs_ = tt % 2
                    ACT.wait(tct, cstfree[cs_])
                    tcc = ACT.mark(nc.scalar.activation(out=cst[cs_][:], in_=ptc[:, 0:384].rearrange("p (g t) -> p g t", g=3),
                                                       func=AF.Copy))
                    ptcfree[0] = tcc
                    POOL.wait(tcc)
                    cstfree[cs_] = css_[cs_].start(
                        POOL, cqkT[:, tt * 128:(tt + 1) * 128].rearrange("(g p) t -> p g t", p=128), cst[cs_][:])

                pend_c = []
                cqn_done = [None, None]
                fm_split = [(0, 1, 2), (3, 4, 5), (6, 7, 8), (9, 10)]
                for step in range(NQ + 1):
                    if step < NQ:
                        for tt in range(4 * step, 4 * step + 4):
                            stepA(tt)
                    if step >= 1:
                        j = step - 1
                        for i_, tt in enumerate(range(4 * j, 4 * j + 4)):
                            proj_tm(tt)
                            if len(pend_c) > 1:
                                proj_tm_finish()
                            proj_fm(j, fm_split[i_])
                    if step < NQ:
                        for tt in range(4 * step, 4 * step + 4):
                            transA(tt)
                while pend_c:
                    proj_tm_finish()
                barrier()


        def phase3(l):
            lam_init = 0.8 - 0.6 * math.exp(-0.3 * l)
            with ExitStack() as es:
                n = lambda s_: f"{s_}_p3_{l}"
                Q = [sbt(es, n(f"Q{i}"), [128, 2, S], BF16) for i in range(2)]
                Kb = [sbt(es, n(f"K{i}"), [128, 4, S], BF16) for i in range(2)]
                Vb = [sbt(es, n(f"V{i}"), [128, NTT, 256], BF16) for i in range(2)]
                NE = 5
                et = [sbt(es, n(f"et{i}"), [128, 1024], BF16) for i in range(NE)]
                zsum = [sbt(es, n(f"zsum{i}"), [128, 512], F32) for i in range(2)]
                zhi = [sbt(es, n(f"zhi{i}"), [128, 512], BF16) for i in range(2)]
                zlo = [sbt(es, n(f"zlo{i}"), [128, 512], BF16) for i in range(2)]
                rz = sbt(es, n("rz"), [128, 512], F32)
                lz = sbt(es, n("lz"), [128, 512], F32)
                ls = sbt(es, n("ls"), [128, 512], F32)
                rsb = sbt(es, n("rsb"), [128, 512], F32)
                o1 = sbt(es, n("o1"), [128, 512], F32)
                o2 = sbt(es, n("o2"), [128, 512], F32)
                obuf = [sbt(es, n(f"obuf{i}"), [128, 512], F32) for i in range(2)]
                sqb = [sbt(es, n(f"sqb{i}"), [128, 512], BF16) for i in range(2)]
                obs = [sbt(es, n(f"obs{i}"), [128, 512], BF16) for i in range(2)]
                diag = sbt(es, n("diag"), [128, 4, 128], BF16)
                biasb = sbt(es, n("biasb"), [128, 4, 384], BF16)
                lq = [sbt(es, n(f"lq{i}"), [128, 64], F32) for i in range(4)]
                lprod = [sbt(es, n(f"lprod{i}"), [128, 64], F32) for i in range(2)]
                lsum = sbt(es, n("lsum"), [128, 2], F32)
                lexp = sbt(es, n("lexp"), [128, 2], F32)
                neglam = sbt(es, n("neglam"), [128, 1], F32)
                gsub = sbt(es, n("gsub"), [128, 1], F32)
                sinkraw = sbt(es, n("sinkraw"), [128, 2], F32)
                sinkexp = sbt(es, n("sinkexp"), [128, 2], F32)
                pss = [pst(es, n(f"pss{i}"), [128, 1024], F32) for i in range(2)]
                acc = [pst(es, n(f"acc{i}"), [128, 512], F32) for i in range(2)]
                zac = [pst(es, n(f"zac{i}"), [128, 512], F32) for i in range(2)]

                csl = slot(n("c"))
                jsl = [slot(n(f"j{i}")) for i in range(2)]
                obsl = [slot(n(f"ob{i}")) for i in range(2)]
                V = nc.vector

                csl.start(SP, diag[:], C["diaga"][:, :, :].rearrange("h s q -> s h q"))
                csl.start(SP, biasb[:], C["biasb"][:, :, :].rearrange("h s q -> s h q"))
                for i, nm in enumerate(("lam_q1", "lam_k1", "lam_q2", "lam_k2")):
                    csl.start(SP, lq[i][:], P[nm][l:l + 1, :].partition_broadcast(128))
                csl.start(SP, gsub[:], P["diff_subln_g"][l:l + 1, :].rearrange("o p -> p o"))
                for p in range(2):
                    csl.start(SP, sinkraw[0:64, p:p + 1], P["sink_logits"][l:l + 1, 2 * p:2 * p + 1].partition_broadcast(64))
                    t_c = csl.start(SP, sinkraw[64:128, p:p + 1],
                                    P["sink_logits"][l:l + 1, 2 * p + 1:2 * p + 2].partition_broadcast(64))
                DVE.wait(t_c)
                DVE.mark(V.tensor_tensor(out=lprod[0][:], in0=lq[0][:], in1=lq[1][:], op=ALU.mult))
                DVE.mark(V.tensor_tensor(out=lprod[1][:], in0=lq[2][:], in1=lq[3][:], op=ALU.mult))
                DVE.dep()
                DVE.mark(V.tensor_reduce(out=lsum[:, 0:1], in_=lprod[0][:], axis=AX.X, op=ALU.add))
                tl = DVE.mark(V.tensor_reduce(out=lsum[:, 1:2], in_=lprod[1][:], axis=AX.X, op=ALU.add))
                ACT.wait(tl, t_c)
                ACT.mark(nc.scalar.activation(out=lexp[:], in_=lsum[:], func=AF.Exp))
                te = ACT.mark(nc.scalar.activation(out=sinkexp[:], in_=sinkraw[:], func=AF.Exp))
                DVE.wait(te)
                DVE.mark(V.tensor_tensor(out=neglam[:], in0=lexp[:, 1:2], in1=lexp[:, 0:1], op=ALU.subtract))
                DVE.dep()
                DVE.mark(V.tensor_scalar(out=neglam[:], in0=neglam[:], scalar1=-lam_init, scalar2=None, op0=ALU.add))
                t_small = DVE.mark(V.tensor_scalar(out=gsub[:], in0=gsub[:], scalar1=1.0 - lam_init, scalar2=None, op0=ALU.mult))
                POOL.mark(nc.gpsimd.memset(Vb[0][:, :, 128:192], 0.0))
                t_vz = POOL.mark(nc.gpsimd.memset(Vb[1][:, :, 128:192], 0.0))

                jobs = [("A", 0), ("A", 1), ("A", 2), ("A", 3), ("C", 0), ("C", 1), ("B", 0), ("B", 1)]
                job_load_tok = [None] * len(jobs)
                job_done_tok = [None] * len(jobs)
                job_zero_tok = [None] * len(jobs)

                def load_job(ji):
                    kind, idx = jobs[ji]
                    sl = ji % 2
                    prev = job_done_tok[ji - 2] if ji >= 2 else None
                    SP.wait(prev)
                    q_, k_, v_ = Q[sl], Kb[sl], Vb[sl]
                    js = jsl[sl]
                    if kind == "A":
                        h = idx
                        for m in range(2):
                            js.start(SP, q_[0:4, m, :], C["alq"][h, :, :])
                            js.start(SP, q_[4:68, m, :], qkT[h * 128 + m * 64:h * 128 + (m + 1) * 64, :])
                            for v, nm in enumerate(("alka", "alkb")):
                                js.start(SP, k_[0:4, 2 * m + v, :], C[nm][h, :, :])
                                js.start(SP, k_[4:68, 2 * m + v, :], qkT[512 + h * 128 + m * 64:512 + h * 128 + (m + 1) * 64, :])
                        tok = js.start(SP, v_[:, :, 0:128], vscr[:, h * 128:(h + 1) * 128].rearrange("(c p) e -> p c e", p=128))
                    else:
                        p = idx
                        if kind == "C":
                            qsrc = cqkT[p * 128:(p + 1) * 128, :]
                            ksrc = cqkT[256 + p * 64:256 + (p + 1) * 64, :]
                            vcol = 640 + p * 64
                        else:
                            qsrc = qkT[1024 + p * 128:1024 + (p + 1) * 128, :]
                            ksrc = qkT[1280 + p * 64:1280 + (p + 1) * 64, :]
                            vcol = 512 + p * 64
                        if kind == "C":
                            POOL.wait(prev)
                            POOL.mark(nc.gpsimd.memset(v_[:, :, 64:128], 0.0))
                            POOL.mark(nc.gpsimd.memset(q_[64:128, 0, :], 0.0))
                            job_zero_tok[ji] = POOL.mark(nc.gpsimd.memset(q_[0:64, 1, :], 0.0))
                        js.start(SP, q_[0:64, 0, :], qsrc[0:64, :])
                        js.start(SP, q_[64:128, 1, :], qsrc[64:128, :])
                        js.start(SP, k_[0:64, 0, :], ksrc)
                        js.start(SP, k_[64:128, 0, :], ksrc)
                        vsrc = vscr[:, vcol:vcol + 64].rearrange("(c p) e -> p c e", p=128)
                        js.start(SP, v_[:, :, 0:64], vsrc)
                        tok = js.start(SP, v_[:, :, 192:256], vsrc)
                    job_load_tok[ji] = tok

                steps = []
                units = []

                def mm(out, lhsT, rhs, start, stop):
                    return nc.tensor.matmul(out, lhsT=lhsT, rhs=rhs, start=start, stop=stop)

                def add_unit(**kw):
                    kw["idx"] = len(units)
                    kw["steps"] = []
                    units.append(kw)
                    return kw

                def add_step(u, **kw):
                    kw["unit"] = u
                    kw["i"] = len(steps)
                    u["steps"].append(kw["i"])
                    steps.append(kw)

                def full_exp(pb, eb):
                    return et[eb][:], pss[pb][:]

                for ji, (kind, idx) in enumerate(jobs):
                    sl = ji % 2
                    if kind == "A":
                        h = idx
                        for j in range(NQ):
                            for m in range(2):
                                u = add_unit(kind="A", job=ji, h=h, j=j, m=m, zf=ones[:])

                                def S_tile(pb, off, c, sl=sl, h=h, j=j, m=m):
                                    q_, k_ = Q[sl], Kb[sl]
                                    if c > 4 * j + 3 or c < 4 * j:
                                        v = 0 if c > 4 * j + 3 else 1
                                        return mm(pss[pb][:, off:off + 512], k_[0:68, 2 * m + v, c * 128:(c + 1) * 128],
                                                  q_[0:68, m, j * 512:(j + 1) * 512], True, True)
                                    r = c - 4 * j
                                    ins = None
                                    for qb in range(4):
                                        cols = slice(off + qb * 128, off + (qb + 1) * 128)
                                        qc = slice(j * 512 + qb * 128, j * 512 + (qb + 1) * 128)
                                        v = 1 if qb > r else 0
                                        ins = mm(pss[pb][:, cols], k_[0:68, 2 * m + v, c * 128:(c + 1) * 128],
                                                 q_[0:68, m, qc], True, qb != r)
                                        if qb == r:
                                            ins = mm(pss[pb][:, cols], ident[:], diag[:, h, :], False, True)
                                    return ins
                                for s_ in range(NTT // 2):
                                    def S_fn(pb, s_=s_, S_tile=S_tile):
                                        S_tile(pb, 0, 2 * s_)
                                        return S_tile(pb, 512, 2 * s_ + 1)

                                    def PV_fn(eb, ub, first, last, sl=sl, s_=s_):
                                        c0, c1 = 2 * s_, 2 * s_ + 1
                                        mm(acc[ub][:], Vb[sl][:, c0, 0:128], et[eb][:, 0:512], first, False)
                                        mm(zac[ub][:], ones[:], et[eb][:, 0:512], first, False)
                                        return mm(acc[ub][:], Vb[sl][:, c1, 0:128], et[eb][:, 512:1024], False, last)
                                    add_step(u, S=S_fn, PV=PV_fn, exp=full_exp, dacc=lambda eb: et[eb][:, 512:1024])
                    elif kind == "C":
                        p = idx
                        for j in range(NQ):
                            u = add_unit(kind="C", job=ji, p=p, j=j, zf=onesp[:, 1, :])
                            for c in range(NTT):
                                def S_fn(pb, sl=sl, j=j, c=c):
                                    mm(pss[pb][:, 0:512], Kb[sl][:, 0, c * 128:(c + 1) * 128], Q[sl][:, 0, j * 512:(j + 1) * 512], True, True)
                                    return mm(pss[pb][:, 512:1024], Kb[sl][:, 0, c * 128:(c + 1) * 128],
                                              Q[sl][:, 1, j * 512:(j + 1) * 512], True, True)

                                def PV_fn(eb, ub, first, last, sl=sl, c=c):
                                    mm(acc[ub][:], Vb[sl][:, c, 0:128], et[eb][:, 0:512], first, False)
                                    mm(zac[ub][:], onesp[:, 0, :], et[eb][:, 0:512], first, False)
                                    return mm(acc[ub][:], Vb[sl][:, c, 128:256], et[eb][:, 512:1024], False, last)
                                add_step(u, S=S_fn, PV=PV_fn, exp=full_exp, dacc=lambda eb: et[eb][:, 512:1024])
                    else:
                        p = idx
                        for j in range(NQ):
                            u = add_unit(kind="B", job=ji, p=p, j=j, zf=None)
                            for qb in range(4 * j, 4 * j + 4):
                                rels = [r for r in (-1, 0, 1) if 0 <= qb + r < NTT]

                                def S_fn(pb, sl=sl, p=p, qb=qb, rels=rels):
                                    ins = None
                                    for hh in range(2):
                                        for r in rels:
                                            ri = r + 1
                                            cols = slice(hh * 512 + ri * 128, hh * 512 + (ri + 1) * 128)
                                            mm(pss[pb][:, cols], Kb[sl][:, 0, (qb + r) * 128:(qb + r + 1) * 128],
                                               Q[sl][:, hh, qb * 128:(qb + 1) * 128], True, False)
                                            ins = mm(pss[pb][:, cols], ident[:], biasb[:, 2 * p + hh, ri * 128:(ri + 1) * 128], False, True)
                                    return ins

                                def PV_fn(eb, ub, first, last, sl=sl, j=j, qb=qb, rels=rels):
                                    ins = None
                                    qc = slice((qb - 4 * j) * 128, (qb - 4 * j + 1) * 128)
                                    for hh in range(2):
                                        for r in rels:
                                            ri = r + 1
                                            cols = slice(hh * 512 + ri * 128, hh * 512 + (ri + 1) * 128)
                                            st_ = (hh == 0 and r == rels[0])
                                            sp_ = (hh == 1 and r == rels[-1])
                                            mm(acc[ub][:, qc], Vb[sl][:, qb + r, hh * 128:(hh + 1) * 128], et[eb][:, cols], st_, sp_)
                                            ins = mm(zac[ub][:, qc], onesp[:, hh, :], et[eb][:, cols], st_, sp_)
                                    return ins

                                def exp_fn(pb, eb, rels=rels):
                                    lo, hi = (rels[0] + 1) * 128, (rels[-1] + 2) * 128
                                    return (et[eb][:].rearrange("p (t c) -> p t c", t=2)[:, :, lo:hi],
                                            pss[pb][:].rearrange("p (t c) -> p t c", t=2)[:, :, lo:hi])
                                add_step(u, S=S_fn, PV=PV_fn, exp=exp_fn, dacc=None)

                NS_ = len(steps)
                tS = [None] * NS_
                tE = [None] * NS_
                tD = [None] * NS_
                accfree = [None, None]
                zsumfree = [None, None]
                zhfree = [None, None]
                zsplit = [None, None]
                sqbfree = [None, None]
                obsfree = [None, None]
                deferred = {}
                cnt = {"sq": 0, "obs": 0}
                first_step_of_job = {}
                for st_ in steps:
                    first_step_of_job.setdefault(st_["unit"]["job"], st_["i"])

                def defer(t, fn):
                    deferred.setdefault(min(t, NS_ - 1), []).append(fn)

                rzfree = [None]
                rsfree = [None]
                ofree = [None, None]

                def recip_on_act(src, bias=0.0):
                    ACT.mark(nc.scalar.activation(out=lz[:], in_=src, func=AF.Ln, bias=bias))
                    ACT.dep()
                    ACT.wait(rzfree[0])
                    return ACT.mark(nc.scalar.activation(out=rz[:], in_=lz[:], func=AF.Exp, scale=-1.0))

                def epilogue(u, tZ):
                    ub = u["idx"] % 2
                    j = u["j"]
                    js_ = slice(j * 512, (j + 1) * 512)
                    ACT.wait(tZ, t_small)
                    if u["kind"] == "A":
                        tr = recip_on_act(zac[ub][:])
                        DVE.wait(tr)
                        if u["m"] == 0:
                            t1 = DVE.mark(V.tensor_tensor(out=o1[:], in0=acc[ub][:], in1=rz[:], op=ALU.mult))
                            rzfree[0] = t1
                            accfree[ub] = [t1]
                            return
                        t2 = DVE.mark(V.tensor_tensor(out=o2[:], in0=acc[ub][:], in1=rz[:], op=ALU.mult))
                        rzfree[0] = t2
                        k = cnt["sq"] % 2
                        cnt["sq"] += 1
                        DVE.dep()
                        DVE.wait(ofree[k])
                        to = DVE.mark(V.scalar_tensor_tensor(out=obuf[k][:], in0=o2[:], scalar=neglam[:, 0:1], in1=o1[:],
                                                             op0=ALU.mult, op1=ALU.add))
                        POOL.wait(to, sqbfree[k])
                        tq = POOL.mark(nc.gpsimd.tensor_tensor(out=sqb[k][:], in0=obuf[k][:], in1=obuf[k][:], op=ALU.mult))
                        h = u["h"]
                        holder = {}

                        def ssq_pe(k=k, tq=tq, ub=ub, holder=holder):
                            PE.wait(tq)
                            holder["tm"] = PE.mark(mm(zac[ub][:], ones[:], sqb[k][:], True, True))
                            sqbfree[k] = holder["tm"]

                        def subln(k=k, ub=ub, h=h, js_=js_, t2=t2, holder=holder):
                            ACT.wait(holder["tm"])
                            tl = ACT.mark(nc.scalar.activation(out=ls[:], in_=zac[ub][:], func=AF.Ln, scale=1.0 / 128, bias=EPS))
                            accfree[ub] = [t2, tl]
                            ACT.dep()
                            ACT.wait(rsfree[0])
                            ts_ = ACT.mark(nc.scalar.activation(out=rsb[:], in_=ls[:], func=AF.Exp, scale=-0.5))
                            kk = cnt["obs"] % 2
                            cnt["obs"] += 1
                            DVE.wait(ts_, obsfree[kk])
                            tn = DVE.mark(V.scalar_tensor_tensor(out=obs[kk][:], in0=obuf[k][:], scalar=gsub[:, 0:1],
                                                                 in1=rsb[:], op0=ALU.mult, op1=ALU.mult))
                            rsfree[0] = tn
                            ofree[k] = tn
                            POOL.wait(tn)
                            obsfree[kk] = obsl[kk].start(POOL, mixT[h * 128:(h + 1) * 128, js_], obs[kk][:])
                        accfree[ub] = [t2]
                        defer(u["steps"][-1] + 6, ssq_pe)
                        defer(u["steps"][-1] + 8, subln)
                        return
                    p = u["p"]
                    if u["kind"] == "B":
                        tr = recip_on_act(zac[ub][:], bias=sinkexp[:, p:p + 1])
                        rowbase = 512 + p * 128
                    else:
                        tr = recip_on_act(zac[ub][:])
                        rowbase = 768 + p * 128
                    k = cnt["obs"] % 2
                    cnt["obs"] += 1
                    DVE.wait(tr, obsfree[k])
                    tb = DVE.mark(V.tensor_tensor(out=obs[k][:], in0=acc[ub][:], in1=rz[:], op=ALU.mult))
                    rzfree[0] = tb
                    accfree[ub] = [tb]
                    POOL.wait(tb)
                    obsfree[k] = obsl[k].start(POOL, mixT[rowbase:rowbase + 128, js_], obs[k][:])

                load_job(0)
                load_job(1)
                PE.wait(t_ident, t_ones, t_c, t_vz)
                for i in range(NS_ + 2):
                    if i < NS_:
                        st_ = steps[i]
                        u = st_["unit"]
                        ub = u["idx"] % 2
                        ji = u["job"]
                        first = (u["steps"][0] == i)
                        if first_step_of_job[ji] == i:
                            PE.wait(job_load_tok[ji], job_zero_tok[ji])
                        if i >= 2:
                            PE.wait(tE[i - 2])
                        if i >= NE:
                            PE.wait(*[t for t in tD[max(0, i - NE - 1):i - NE + 1] if t is not None][-1:])
                        tS[i] = PE.mark(st_["S"](i % 2))
                        ACT.wait(tS[i])
                        eo, ei = st_["exp"](i % 2, i % NE)
                        tE[i] = ACT.mark(nc.scalar.activation(out=eo, in_=ei, func=AF.Exp, scale=0.125))
                        if st_["dacc"] is not None:
                            DVE.wait(tE[i])
                            src = st_["dacc"](i % NE)
                            if first:
                                DVE.wait(zsumfree[ub])
                                tD[i] = DVE.mark(V.tensor_copy(out=zsum[ub][:], in_=src))
                            else:
                                DVE.dep()
                                tD[i] = DVE.mark(V.tensor_tensor(out=zsum[ub][:], in0=zsum[ub][:], in1=src, op=ALU.add))
                            if u["steps"][-1] == i:
                                DVE.dep()
                                DVE.wait(zhfree[ub])
                                DVE.mark(V.tensor_copy(out=zhi[ub][:], in_=zsum[ub][:]))
                                DVE.dep()
                                zsplit[ub] = DVE.mark(V.tensor_tensor(out=zlo[ub][:], in0=zsum[ub][:], in1=zhi[ub][:],
                                                                      op=ALU.subtract))
                    j_ = i - 2
                    if j_ >= 0:
                        st_ = steps[j_]
                        u = st_["unit"]
                        ub = u["idx"] % 2
                        PE.wait(tE[j_])
                        first = (u["steps"][0] == j_)
                        last = (u["steps"][-1] == j_)
                        if first:
                            PE.wait(*(accfree[ub] or []))
                        ins = st_["PV"](j_ % NE, ub, first, last)
                        if last:
                            tPV = PE.mark(ins)
                            job_done_tok[u["job"]] = tPV
                            if u["zf"] is not None:
                                hz = {}

                                def zfin(u=u, ub=ub, td=zsplit[ub], hz=hz):
                                    PE.wait(td)
                                    mm(zac[ub][:], u["zf"], zhi[ub][:], False, False)
                                    hz["tz"] = PE.mark(mm(zac[ub][:], u["zf"], zlo[ub][:], False, True))
                                    zhfree[ub] = hz["tz"]
                                defer(j_ + 1, zfin)
                                defer(j_ + 2, lambda u=u, hz=hz: epilogue(u, hz["tz"]))
                            else:
                                defer(j_ + 1, lambda u=u, tPV=tPV: epilogue(u, tPV))
                            ji = u["job"]
                            if j_ + 1 < NS_ and steps[j_ + 1]["unit"]["job"] != ji and ji + 2 < len(jobs):
                                load_job(ji + 2)
                        for fn in deferred.pop(j_, []):
                            fn()
                for k_ in sorted(deferred):
                    for fn in deferred[k_]:
                        fn()
                barrier()

        def post_norm_store(st, tt, pyA, pyB, tP, xt_tile, t_x):
            V = nc.vector
            ACT.wait(tP)
            ACT.mark(nc.scalar.activation(out=st["junk"][:, 0:512], in_=pyA, func=AF.Square, accum_out=st["ss2"][:, 2 * tt:2 * tt + 1]))
            a2 = ACT.mark(nc.scalar.activation(out=st["junk"][:, 512:1024], in_=pyB, func=AF.Square,
                                               accum_out=st["ss2"][:, 2 * tt + 1:2 * tt + 2]))
            DVE.wait(a2)
            d1 = DVE.mark(V.tensor_tensor(out=st["ss"][:, tt:tt + 1], in0=st["ss2"][:, 2 * tt:2 * tt + 1],
                                          in1=st["ss2"][:, 2 * tt + 1:2 * tt + 2], op=ALU.add))
            ACT.wait(d1)
            a3 = ACT.mark(nc.scalar.activation(out=st["sd"][:, tt:tt + 1], in_=st["ss"][:, tt:tt + 1], func=AF.Sqrt,
                                               scale=1.0 / D, bias=EPS))
            DVE.wait(a3)
            DVE.mark(V.reciprocal(out=st["rstd"][:, tt:tt + 1], in_=st["sd"][:, tt:tt + 1]))
            DVE.dep()
            k = st["yoi"] % len(st["yo"])
            st["yoi"] += 1
            yo = st["yo"][k]
            DVE.wait(st["yofree"][k], t_x, st["t_g"])
            DVE.mark(V.scalar_tensor_tensor(out=yo[:, 0:512], in0=pyA, scalar=st["rstd"][:, tt:tt + 1], in1=st["gpost"][:, 0:512],
                                            op0=ALU.mult, op1=ALU.mult))
            tfree = DVE.mark(V.scalar_tensor_tensor(out=yo[:, 512:1024], in0=pyB, scalar=st["rstd"][:, tt:tt + 1],
                                                    in1=st["gpost"][:, 512:1024], op0=ALU.mult, op1=ALU.mult))
            DVE.dep()
            tdone = DVE.mark(V.tensor_tensor(out=yo[:], in0=yo[:], in1=xt_tile[:], op=ALU.add))
            POOL.wait(tdone)
            st["yofree"][k] = st["yosl"][k].start(POOL, y[tt * 128:(tt + 1) * 128, :], yo[:])
            return tfree, tdone

        def load_wout(l, es):
            wout = sbt(es, f"wout_p4_{l}", [128, 8, D], BF16)
            wsl = slot(f"w_p4_{l}")
            for c in range(8):
                t_w = wsl.start(POOL, wout[:, c, :], P["w_out"][l, c * 128:(c + 1) * 128, :])
            return wout, t_w

        def phase4(l, wout, t_w):
            xsrc = x if l == 0 else y
            with ExitStack() as es:
                n = lambda s_: f"{s_}_p4_{l}"
                st = {
                    "gpost": sbt(es, n("gpost"), [128, D], F32),
                    "junk": sbt(es, n("junk"), [128, D], BF16),
                    "ss2": sbt(es, n("ss2"), [128, 2 * NTT], F32),
                    "ss": sbt(es, n("ss"), [128, NTT], F32),
                    "sd": sbt(es, n("sd"), [128, NTT], F32),
                    "rstd": sbt(es, n("rstd"), [128, NTT], F32),
                    "yo": [sbt(es, n(f"yo{i}"), [128, D], F32) for i in range(4)],
                    "yofree": [None] * 4, "yoi": 0,
                    "yosl": [slot(n(f"yo{i}")) for i in range(4)],
                }
                mixt = [sbt(es, n(f"mixt{i}"), [128, 8, 512], BF16) for i in range(2)]
                xt = [sbt(es, n(f"xt{i}"), [128, D], F32) for i in range(6)]
                py = [pst(es, n(f"py{i}"), [128, 512], F32) for i in range(8)]
                gsl = slot(n("g"))
                msl = [slot(n(f"m{i}")) for i in range(2)]
                xs = [slot(n(f"x{i}")) for i in range(6)]
                st["t_g"] = gsl.start(SP, st["gpost"][:], P["g_post_mix"][l:l + 1, :].partition_broadcast(128))
                mfree = [None, None]
                xfree = [None] * 6
                pyfree = [None] * 4
                for grp in range(NQ):
                    mk = grp % 2
                    SP.wait(mfree[mk])
                    t_m = msl[mk].start(SP, mixt[mk][:], mixT[:, grp * 512:(grp + 1) * 512].rearrange("(c p) t -> p c t", p=128))
                    for ti in range(4):
                        tt = grp * 4 + ti
                        sl = tt % 6
                        SP.wait(xfree[sl])
                        t_x = xs[sl].start(SP, xt[sl][:], xsrc[tt * 128:(tt + 1) * 128, :])
                        pk = tt % 4
                        PE.wait(t_m, t_w, pyfree[pk])
                        for nh in range(2):
                            for c in range(8):
                                ins = nc.tensor.matmul(py[2 * pk + nh][:], lhsT=mixt[mk][:, c, ti * 128:(ti + 1) * 128],
                                                       rhs=wout[:, c, nh * 512:(nh + 1) * 512], start=(c == 0), stop=(c == 7))
                        tP = PE.mark(ins)
                        if ti == 3:
                            mfree[mk] = tP
                        pyfree[pk], xfree[sl] = post_norm_store(st, tt, py[2 * pk][:], py[2 * pk + 1][:], tP, xt[sl], t_x)
                barrier()

        def mlp_weights(l, es):
            n = lambda s_: f"{s_}_p5_{l}"
            w1 = sbt(es, n("w1"), [128, 8, DFF], BF16)
            w2 = sbt(es, n("w2"), [128, 32, D], BF16)
            hold = {}
            w1sl = [slot(n(f"w1b{i}")) for i in range(8)]
            w2sl = [slot(n(f"w2b{i}")) for i in range(8)]

            def issue():
                pass

            def issue2():
                hold["t_w1"] = [w1sl[i].start(POOL, w1[:, :, i * 512:(i + 1) * 512],
                                              P["w_mlp_in"][l, :, i * 512:(i + 1) * 512].rearrange("(c p) n -> p c n", p=128))
                                for i in range(8)]
                hold["t_w2"] = [w2sl[i].start(POOL, w2[:, 4 * i:4 * i + 4, :],
                                              P["w_mlp_out"][l, i * 512:(i + 1) * 512, :].rearrange("(f p) n -> p f n", p=128))
                                for i in range(8)]
            hold["issue2"] = issue2
            return w1, w2, issue, hold

        def phase5(l, w1, w2, hold_w):
            hold_w["issue2"]()
            t_w1b, t_w2b = hold_w["t_w1"], hold_w["t_w2"]
            with ExitStack() as es:
                n = lambda s_: f"{s_}_p5_{l}"
                gpre = sbt(es, n("gpre"), [128, D], F32)
                st = {
                    "gpost": sbt(es, n("gpost"), [128, D], F32),
                    "junk": sbt(es, n("junk"), [128, D], BF16),
                    "ss2": sbt(es, n("ss2"), [128, 2 * NTT], F32),
                    "ss": sbt(es, n("ss"), [128, NTT], F32),
                    "sd": sbt(es, n("sd"), [128, NTT], F32),
                    "rstd": sbt(es, n("rstd"), [128, NTT], F32),
                    "yo": [sbt(es, n(f"yo{i}"), [128, D], F32) for i in range(2)],
                    "yofree": [None, None], "yoi": 0,
                    "yosl": [slot(n(f"yo{i}")) for i in range(2)],
                }
                ssp = sbt(es, n("ssp"), [128, NTT], F32)
                sdp = sbt(es, n("sdp"), [128, NTT], F32)
                rsp = sbt(es, n("rsp"), [128, NTT], F32)
                uT = sbt(es, n("uT"), [128, 32, 512], BF16)
                h2T = sbt(es, n("h2T"), [128, 8, 512], BF16)
                hb = [sbt(es, n(f"hb{i}"), [128, D], BF16) for i in range(2)]
                xt = [sbt(es, n(f"xt{i}"), [128, D], F32) for i in range(3)]
                ptr = [pst(es, n(f"ptr{i}"), [128, D], BF16) for i in range(2)]
                pu = [pst(es, n(f"pu{i}"), [128, 512], F32) for i in range(2)]
                py = [pst(es, n(f"py{i}"), [128, 512], F32) for i in range(4)]
                gsl = slot(n("g"))
                xs = [slot(n(f"x{i}")) for i in range(3)]
                V = nc.vector
                gsl.start(SP, gpre[:], P["g_pre_mlp"][l:l + 1, :].partition_broadcast(128))
                st["t_g"] = gsl.start(SP, st["gpost"][:], P["g_post_mlp"][l:l + 1, :].partition_broadcast(128))
                xfree = [None] * 3
                hbfree = [None, None]
                ptrfree = [None, None]
                pufree = [None, None]
                pyfree = [None, None]
                h2free = [None]
                uTfree = [None]
                xi = {"n": 0}

                def load_x(tt):
                    sl = xi["n"] % 3
                    xi["n"] += 1
                    SP.wait(xfree[sl])
                    return sl, xs[sl].start(SP, xt[sl][:], y[tt * 128:(tt + 1) * 128, :])

                hb_tok = {}

                def prep0(tt):
                    sl, t_x = load_x(tt)
                    ACT.wait(t_x, hbfree[tt % 2])
                    t1 = ACT.mark(nc.scalar.activation(out=hb[tt % 2][:], in_=xt[sl][:], func=AF.Square, accum_out=ssp[:, tt:tt + 1]))
                    ACT.wait(t1)
                    t2 = ACT.mark(nc.scalar.activation(out=sdp[:, tt:tt + 1], in_=ssp[:, tt:tt + 1], func=AF.Sqrt, scale=1.0 / D, bias=EPS))
                    DVE.wait(t2)
                    DVE.mark(V.reciprocal(out=rsp[:, tt:tt + 1], in_=sdp[:, tt:tt + 1]))
                    DVE.dep()
                    DVE.wait(hbfree[tt % 2], st["t_g"], t_x)
                    t4 = DVE.mark(V.scalar_tensor_tensor(out=hb[tt % 2][:], in0=xt[sl][:], scalar=rsp[:, tt:tt + 1], in1=gpre[:],
                                                         op0=ALU.mult, op1=ALU.mult))
                    xfree[sl] = t4
                    hb_tok[tt] = t4

                def trans0(tt):
                    ti = tt % 4
                    PE.wait(hb_tok[tt], ptrfree[tt % 2], t_ident)
                    for c in range(8):
                        ins = nc.tensor.transpose(ptr[tt % 2][:, c * 128:(c + 1) * 128], hb[tt % 2][:, c * 128:(c + 1) * 128], ident[:])
                    t5 = PE.mark(ins)
                    hbfree[tt % 2] = t5
                    ACT.wait(t5, h2free[0])
                    t6 = ACT.mark(nc.scalar.activation(out=h2T[:, :, ti * 128:(ti + 1) * 128],
                                                       in_=ptr[tt % 2][:].rearrange("p (c t) -> p c t", c=8), func=AF.Copy))
                    ptrfree[tt % 2] = t6
                    return t6

                def stage0(grp):
                    t6 = None
                    for ti in range(4):
                        prep0(grp * 4 + ti)
                        t6 = trans0(grp * 4 + ti)
                    return t6

                def stage1(grp, t_h2):
                    PE.wait(t_h2)
                    last = None
                    for f in range(32):
                        b = f % 2
                        PE.wait(pufree[b], t_w1b[f // 4])
                        if f == 0:
                            PE.wait(uTfree[0])
                        for c in range(8):
                            ins = nc.tensor.matmul(pu[b][:], lhsT=w1[:, c, f * 128:(f + 1) * 128], rhs=h2T[:, c, :],
                                                   start=(c == 0), stop=(c == 7))
                        tP = PE.mark(ins)
                        ACT.wait(tP, uTfree[0])
                        ta = ACT.mark(nc.scalar.activation(out=uT[:, f, :], in_=pu[b][:], func=AF.Square))
                        DVE.wait(ta)
                        last = DVE.mark(V.scalar_tensor_tensor(out=uT[:, f, :], in0=pu[b][:], scalar=0.0, in1=uT[:, f, :],
                                                               op0=ALU.is_gt, op1=ALU.mult))
                        pufree[b] = last
                    h2free[0] = tP
                    return last

                def stage2(grp, t_u, tiles=(0, 1, 2, 3)):
                    PE.wait(t_u)
                    for ti in tiles:
                        tt = grp * 4 + ti
                        sl, t_x = load_x(tt)
                        pk = tt % 2
                        PE.wait(pyfree[pk])
                        for nh in range(2):
                            for f in range(32):
                                if f % 4 == 0:
                                    PE.wait(t_w2b[f // 4])
                                ins = nc.tensor.matmul(py[2 * pk + nh][:], lhsT=uT[:, f, ti * 128:(ti + 1) * 128],
                                                       rhs=w2[:, f, nh * 512:(nh + 1) * 512], start=(f == 0), stop=(f == 31))
                        tP = PE.mark(ins)
                        if ti == 3:
                            uTfree[0] = tP
                        pyfree[pk], xfree[sl] = post_norm_store(st, tt, py[2 * pk][:], py[2 * pk + 1][:], tP, xt[sl], t_x)

                t_h = stage0(0)
                for grp in range(NQ):
                    t_u = stage1(grp, t_h)
                    if grp + 1 < NQ:
                        g1 = 4 * (grp + 1)
                        prep0(g1)
                        prep0(g1 + 1)
                        stage2(grp, t_u, (0,))
                        trans0(g1)
                        trans0(g1 + 1)
                        prep0(g1 + 2)
                        stage2(grp, t_u, (1,))
                        trans0(g1 + 2)
                        prep0(g1 + 3)
                        stage2(grp, t_u, (2,))
                        t_h = trans0(g1 + 3)
                        stage2(grp, t_u, (3,))
                    else:
                        stage2(grp, t_u)
                barrier()

        for l in layers:
            if "p1" in phases:
                with nc.named_scope(f"L{l}_p1"):
                    phase1(l)
            with ExitStack() as es34:
                if "p4" in phases:
                    wout, t_wout = load_wout(l, es34)
                if "p3" in phases:
                    with nc.named_scope(f"L{l}_p3"):
                        phase3(l)
                if "p4" in phases:
                    with nc.named_scope(f"L{l}_p4"):
                        phase4(l, wout, t_wout)
            with ExitStack() as es45:
                if "p5" in phases:
                    w1, w2, issue_w, hold_w = mlp_weights(l, es45)
                    with nc.named_scope(f"L{l}_p5"):
                        phase5(l, w1, w2, hold_w)
        barrier()
    return nc


_NC_CACHE = {}


def _in_maps(inputs, consts):
    maps = []
    shared = {k: np.ascontiguousarray(np.asarray(inputs[k], dtype=np.float32)) for k in PARAM_SPECS}
    xs = np.asarray(inputs["x"], dtype=np.float32)
    for b in range(8):
        m = {"x": np.ascontiguousarray(xs[b])}
        m.update(shared)
        m.update(consts)
        maps.append(m)
    return maps


def kernel(**inputs):
    if "nc" not in _NC_CACHE:
        _NC_CACHE["nc"] = build()
        _NC_CACHE["consts"] = _consts()
    nc = _NC_CACHE["nc"]
    maps = _in_maps(inputs, _NC_CACHE["consts"])
    res = run_bass_kernel_spmd(nc, maps, core_ids=list(range(8)))
    return np.stack([np.asarray(r["y"], dtype=np.float32) for r in res.results], axis=0)
```
